# Optimizing a Trainium2 kernel written in Bass

```python
import jax, jax.numpy as jnp
from jax import lax
import numpy as np

D_MODEL = 1024
BATCH = 8
SEQ = 2048
DEPTH = 4

N_MIXERS = 4
EXPAND = 2
D_INNER = EXPAND * D_MODEL
FNET_GROUPS = 4
FNET_GROUP_DIM = D_INNER // FNET_GROUPS
CONF_KERNEL = 31
POOL_WINDOWS = (2, 4, 8, 16)
POOL_GROUPS = len(POOL_WINDOWS)
POOL_GROUP_DIM = D_INNER // POOL_GROUPS
SHORT_CONV_WIDTH = 3
RMS_EPS = 1e-6
LN_EPS = 1e-5

kernel_name = "hybrid_fourier_conformer_pool_shortconv_encoder"


def _n_layers_of(m):
    return len(range(m, DEPTH, N_MIXERS))


def rms_norm(x, g):
    xf = x.astype(jnp.float32)
    y = xf * lax.rsqrt(jnp.mean(xf * xf, axis=-1, keepdims=True) + RMS_EPS)
    return (y * g.astype(jnp.float32)).astype(x.dtype)


def depthwise_conv(h, w):
    k = w.shape[0]
    return lax.conv_general_dilated(
        h, w[:, None, :].astype(h.dtype), window_strides=(1,),
        padding=[(k // 2, k // 2)], dimension_numbers=('NWC', 'WIO', 'NWC'),
        feature_group_count=h.shape[-1])


def fourier_mix(u, w_mix, b_mix):
    b_, s_, _ = u.shape
    ug = u.reshape(b_, s_, FNET_GROUPS, FNET_GROUP_DIM).astype(jnp.float32)
    f = jnp.fft.fftn(ug, axes=(1, 3), norm="ortho").real.astype(u.dtype)
    y = jnp.einsum('bsgc,gcd->bsgd', f, w_mix) + b_mix
    return y.reshape(b_, s_, D_INNER)


def conformer_conv(u, dw_w, dw_b, ln_g, ln_b):
    a, gate = jnp.split(u, 2, axis=-1)
    h = a * jax.nn.sigmoid(gate)
    h = depthwise_conv(h, dw_w) + dw_b
    hf = h.astype(jnp.float32)
    mu = jnp.mean(hf, axis=-1, keepdims=True)
    var = jnp.mean(jnp.square(hf - mu), axis=-1, keepdims=True)
    hn = ((hf - mu) * lax.rsqrt(var + LN_EPS) * ln_g.astype(jnp.float32)
          + ln_b.astype(jnp.float32)).astype(u.dtype)
    return jax.nn.silu(hn)


def multiscale_pool(u, w_grp, scale):
    b_, s_, _ = u.shape
    uf = u.astype(jnp.float32)
    cs = jnp.concatenate([jnp.zeros((b_, 1, D_INNER), jnp.float32),
                          jnp.cumsum(uf, axis=1)], axis=1)
    t = jnp.arange(s_)
    outs = []
    for g, w in enumerate(POOL_WINDOWS):
        left = w // 2
        right = w - 1 - left
        lo = jnp.clip(t - left, 0, s_)
        hi = jnp.clip(t + right + 1, 0, s_)
        sl = slice(g * POOL_GROUP_DIM, (g + 1) * POOL_GROUP_DIM)
        csg = cs[..., sl]
        cnt = (hi - lo).astype(jnp.float32)[None, :, None]
        mean = (jnp.take(csg, hi, axis=1) - jnp.take(csg, lo, axis=1)) / cnt
        outs.append(mean - uf[..., sl])
    p = jnp.stack(outs, axis=2).astype(u.dtype)
    y = jnp.einsum('bsgc,gcd->bsgd', p, w_grp).reshape(b_, s_, D_INNER)
    return y * scale


def short_gated_conv(u, conv_w):
    bg, cg, h = jnp.split(u, 3, axis=-1)
    return bg * depthwise_conv(cg * h, conv_w)


def setup_inputs(seed: int = 0) -> dict:
    key = jax.random.key(seed)
    ks = jax.random.split(key, 20)
    f32 = jnp.float32
    nA, nB, nC, nD = (_n_layers_of(m) for m in range(N_MIXERS))
    E, D = D_INNER, D_MODEL
    nrm = lambda k, shape, s: (jax.random.normal(k, shape, f32) * s)
    return {
        "x": jax.random.normal(ks[0], (BATCH, SEQ, D), f32),
        "norm_g": 1.0 + nrm(ks[1], (DEPTH, D), 0.05),
        "w_out": nrm(ks[2], (DEPTH, E, D), E ** -0.5),
        "final_g": 1.0 + nrm(ks[3], (D,), 0.05),
        "fnet_w_in": nrm(ks[4], (nA, D, 2 * E), D ** -0.5),
        "fnet_w_mix": nrm(ks[5], (nA, FNET_GROUPS, FNET_GROUP_DIM, FNET_GROUP_DIM), FNET_GROUP_DIM ** -0.5),
        "fnet_b_mix": nrm(ks[6], (nA, FNET_GROUPS, FNET_GROUP_DIM), 0.02),
        "conf_w_in": nrm(ks[7], (nB, D, 3 * E), D ** -0.5),
        "conf_dw_w": nrm(ks[8], (nB, CONF_KERNEL, E), CONF_KERNEL ** -0.5),
        "conf_dw_b": nrm(ks[9], (nB, E), 0.02),
        "conf_ln_g": 1.0 + nrm(ks[10], (nB, E), 0.05),
        "conf_ln_b": nrm(ks[11], (nB, E), 0.02),
        "pool_w_in": nrm(ks[12], (nC, D, 2 * E), D ** -0.5),
        "pool_w_grp": nrm(ks[13], (nC, POOL_GROUPS, POOL_GROUP_DIM, POOL_GROUP_DIM), POOL_GROUP_DIM ** -0.5),
        "pool_scale": 1.0 + nrm(ks[14], (nC, E), 0.1),
        "sc_w_in": nrm(ks[15], (nD, D, 4 * E), D ** -0.5),
        "sc_conv_w": nrm(ks[16], (nD, SHORT_CONV_WIDTH, E), SHORT_CONV_WIDTH ** -0.5),
    }


def reference(x, norm_g, w_out, final_g, fnet_w_in, fnet_w_mix, fnet_b_mix,
              conf_w_in, conf_dw_w, conf_dw_b, conf_ln_g, conf_ln_b,
              pool_w_in, pool_w_grp, pool_scale, sc_w_in, sc_conv_w):
    for i in range(DEPTH):
        m, j = i % N_MIXERS, i // N_MIXERS
        xn = rms_norm(x, norm_g[i])
        if m == 0:
            hp = jnp.einsum('bsd,de->bse', xn, fnet_w_in[j])
            u = fourier_mix(hp[..., :D_INNER], fnet_w_mix[j], fnet_b_mix[j])
        elif m == 1:
            hp = jnp.einsum('bsd,de->bse', xn, conf_w_in[j])
            u = conformer_conv(hp[..., :2 * D_INNER], conf_dw_w[j], conf_dw_b[j],
                               conf_ln_g[j], conf_ln_b[j])
        elif m == 2:
            hp = jnp.einsum('bsd,de->bse', xn, pool_w_in[j])
            u = multiscale_pool(hp[..., :D_INNER], pool_w_grp[j], pool_scale[j])
        else:
            hp = jnp.einsum('bsd,de->bse', xn, sc_w_in[j])
            u = short_gated_conv(hp[..., :3 * D_INNER], sc_conv_w[j])
        z = hp[..., -D_INNER:]
        x = x + jnp.einsum('bse,ed->bsd', u * jax.nn.silu(z), w_out[i])
    return rms_norm(x, final_g)
```

```python
import contextlib
import os
import numpy as np
import ml_dtypes
import concourse.bass as bass
import concourse.mybir as mybir
from concourse.bass_utils import run_bass_kernel_spmd

F32 = mybir.dt.float32
BF16 = mybir.dt.bfloat16
AF = mybir.ActivationFunctionType
ALU = mybir.AluOpType

S_LEN = 2048
D = 1024
E = 2048
NT = 16
TOT = 106000


class Res:
    __slots__ = ("name", "w", "r")

    def __init__(self, name):
        self.name = name
        self.w = None
        self.r = {}


class Sched:
    ENG = ("pe", "act", "dve", "pool", "sp")

    def __init__(self, nc):
        self.nc = nc
        self.ops = {e: [] for e in self.ENG}
        self.cnt = {}
        self.waited = {e: {} for e in self.ENG}
        for e in ("pe", "act", "dve", "pool"):
            self.cnt[e] = 0

    def _deps(self, eng, reads, writes):
        need = {}
        for r in reads:
            if r.w is not None:
                k, v = r.w
                need[k] = max(need.get(k, 0), v)
        for r in writes:
            if r.w is not None:
                k, v = r.w
                need[k] = max(need.get(k, 0), v)
            for k, v in r.r.items():
                need[k] = max(need.get(k, 0), v)
        waits = []
        wd = self.waited[eng]
        for k, v in need.items():
            if eng == "pe" and k == "pe":
                continue
            if wd.get(k, 0) < v:
                wd[k] = v
                waits.append((k, v))
        return waits

    def _mark(self, key, val, reads, writes):
        for r in reads:
            r.r[key] = max(r.r.get(key, 0), val)
        for r in writes:
            r.w = (key, val)
            r.r = {}

    def op(self, eng, fn, reads=(), writes=(), signal=True):
        waits = self._deps(eng, reads, writes)
        if signal:
            self.cnt[eng] += 1
            val = self.cnt[eng]
        else:
            val = self.cnt[eng] + 1
        self.ops[eng].append((waits, fn, (eng, 1) if signal else None))
        self._mark(eng, val, reads, writes)
        return val

    def dma(self, q, fn, key, reads=(), writes=()):
        if key not in self.cnt:
            self.cnt[key] = 0
        waits = self._deps(q, reads, writes)
        self.cnt[key] += 16
        val = self.cnt[key]
        self.ops[q].append((waits, fn, (key, 16)))
        self._mark(key, val, reads, writes)

    def barrier(self):
        for e in self.ENG:
            waits = []
            for k, v in self.cnt.items():
                if v > 0 and self.waited[e].get(k, 0) < v and not (e == "pe" and k == "pe"):
                    self.waited[e][k] = v
                    waits.append((k, v))
            if waits:
                self.ops[e].append((waits, None, None))

    def emit(self):
        nc = self.nc
        with contextlib.ExitStack() as st:
            sems = {k: st.enter_context(nc.semaphore("s_" + k)) for k in self.cnt}
            self.barrier()
            block = st.enter_context(nc.Block())

            def body(e):
                def run(eng):
                    for waits, fn, inc in self.ops[e]:
                        for k, v in waits:
                            eng.wait_ge(sems[k], v)
                        if fn is None:
                            continue
                        ins = fn(eng)
                        if inc is not None:
                            ins.then_inc(sems[inc[0]], inc[1])
                return run

            block.tensor(body("pe"))
            block.scalar(body("act"))
            block.vector(body("dve"))
            block.gpsimd(body("pool"))
            block.sync(body("sp"))


WIN_COLS = {0: 4096, 1: 6144, 2: 4096, 3: 8192}
V_DW, V_DWB, V_LNG, V_LNB, V_PSC, V_BMIX, V_SC = 0, 31, 32, 33, 34, 35, 36
NV = 40
_STOP = int(os.environ.get('L0_STOP', '9'))
KB_W = 342
KB0 = (0, 342, 684)
KBN = (342, 342, 341)


def build_program(layers, final):
    nc = bass.Bass("TRN2", target_bir_lowering=False)
    dt = nc.dram_tensor
    x_d = dt("x", [S_LEN, D], F32, kind="ExternalInput").ap()
    ng_d = dt("ng", [5, D], F32, kind="ExternalInput").ap()
    wi_d = {L: dt("wi%d" % L, [WIN_COLS[L] // 256, 128, 8, 256], F32, kind="ExternalInput").ap() for L in layers}
    wo_d = dt("wo", [4, 4, 128, 4, 1024], F32, kind="ExternalInput").ap()
    wm_d = dt("wm", [4, 128, 4, 512], F32, kind="ExternalInput").ap()
    wg_d = dt("wg", [4, 128, 4, 512], F32, kind="ExternalInput").ap()
    vec_d = dt("vec", [128, 16 * NV], F32, kind="ExternalInput").ap()
    idb_d = dt("idb", [128, 128], BF16, kind="ExternalInput").ap()
    onef_d = dt("onef", [128, 128], F32, kind="ExternalInput").ap()
    csc_d = dt("csc", [128, 4 * 1024], BF16, kind="ExternalInput").ap()
    css_d = dt("css", [2, 3, 8, 128, KB_W], BF16, kind="ExternalInput").ap()
    pm_d = dt("pm", [128, 4 * 5 * 128], BF16, kind="ExternalInput").ap()
    y_d = dt("y", [S_LEN, D], F32, kind="ExternalOutput").ap()
    hc_d = dt("hc_scr", [E, S_LEN], BF16, kind="Internal").ap() if 1 in layers else None

    with contextlib.ExitStack() as st:
        big = st.enter_context(nc.sbuf_tensor("big", [128, TOT], BF16))
        PS = [st.enter_context(nc.psum_tensor("ps%d" % i, [128, 512], F32)) for i in range(8)]
        PSR = [Res("ps%d" % i) for i in range(8)]
        S = Sched(nc)

        off = [0]

        def carve(n, dtype=BF16):
            units = n * (2 if dtype == F32 else 1)
            a = big[:, off[0]:off[0] + units]
            off[0] += units
            assert off[0] <= TOT, off[0]
            return a.bitcast(F32) if dtype == F32 else a

        X = carve(NT * D, F32).rearrange("p (t d) -> p t d", t=NT)
        XR = [Res("x%d" % t) for t in range(NT)]
        XNT = carve(8 * S_LEN).rearrange("p (c s) -> p c s", c=8)
        XNTR = [Res("xnt%d" % t) for t in range(NT)]
        Gfull = carve(4 * S_LEN)
        G = Gfull.rearrange("p (c s) -> p c s", c=4)
        GR = [Res("g%d" % c) for c in range(4)]
        NWB = 6
        WB = [carve(8 * 256).rearrange("p (c n) -> p c n", c=8) for _ in range(NWB)]
        WBR = [Res("wb%d" % i) for i in range(NWB)]
        WO = [carve(4 * 1024).rearrange("p (c n) -> p c n", c=4) for _ in range(1)]
        WOR = [Res("wo%d" % i) for i in range(1)]
        IDB = carve(128); IDBR = Res("idb")
        ONEF = carve(128, F32); ONEFR = Res("onef")
        VEC = carve(16 * NV, F32).rearrange("p (c r) -> p c r", c=16); VECR = Res("vec")
        GB = [carve(D, F32) for _ in range(1)]
        GBR = [Res("gb%d" % i) for i in range(1)]
        SS = carve(4 * NT, F32).rearrange("p (a t) -> p a t", a=4); SSR = Res("ss")
        scratch0 = off[0]

        wb_rr = [0]

        def load_w(L, chunk):
            k = wb_rr[0] % NWB
            wb_rr[0] += 1
            S.dma("pool", lambda e, k=k: e.dma_start(out=WB[k], in_=wi_d[L][chunk]), "k_wb%d" % k, writes=[WBR[k]])
            return k

        wo_rr = [0]

        def load_wo(L, g):
            k = 0
            S.dma("pool", lambda e, k=k: e.dma_start(out=WO[k], in_=wo_d[L, g]), "k_wo%d" % k, writes=[WOR[k]])
            return k

        bank_rr = [0]

        def bank():
            b = bank_rr[0] % 8
            bank_rr[0] += 1
            return b

        def mm_group(b, out_ap, pairs, reads):
            n = len(pairs)
            for i, (l, r) in enumerate(pairs):
                S.op("pe", lambda e, l=l, r=r, i=i: e.matmul(out_ap, l, r, start=(i == 0), stop=(i == n - 1)),
                     reads=reads, writes=[PSR[b]], signal=(i == n - 1))

        def mm_multi(b, groups, reads):
            tot = sum(len(p) for _, p in groups)
            idx = 0
            for out_ap, pairs in groups:
                n = len(pairs)
                for i, (l, r) in enumerate(pairs):
                    S.op("pe", lambda e, o=out_ap, l=l, r=r, f=(idx == 0), la=(i == n - 1): e.matmul(
                        o, l, r, start=f, stop=la, skip_group_check=True),
                        reads=reads, writes=[PSR[b]], signal=(idx == tot - 1))
                    idx += 1

        S.dma("sp", lambda e: e.dma_start(out=IDB, in_=idb_d[:, :]), "k_c0", writes=[IDBR])
        S.dma("sp", lambda e: e.dma_start(out=ONEF, in_=onef_d[:, :]), "k_c1", writes=[ONEFR])
        S.dma("sp", lambda e: e.dma_start(out=VEC.rearrange("p c r -> p (c r)"), in_=vec_d[:, :]), "k_c2", writes=[VECR])
        xv = x_d.rearrange("(t p) d -> p t d", p=128)
        for q in range(4):
            S.dma("sp", lambda e, q=q: e.dma_start(out=X[:, 4 * q:4 * q + 4, :], in_=xv[:, 4 * q:4 * q + 4, :]),
                  "k_x%d" % q, writes=XR[4 * q:4 * q + 4])
        gb_rr = [0]

        def load_gain(row):
            k = 0
            S.dma("sp", lambda e, k=k: e.dma_start(out=GB[k], in_=ng_d[row:row + 1, :].partition_broadcast(128)),
                  "k_gb%d" % k, writes=[GBR[k]])
            return k

        def rms_stats():
            S.op("dve", lambda e: e.memset(SS[:, 0, :], 0.0), writes=[SSR])

        def norm_and_transpose(gk, junk, junkr, xn_buf, xn_res):
            for t in range(NT):
                S.op("act", lambda e, t=t: e.activation(out=junk, in_=X[:, t, :], func=AF.Square,
                                                        accum_out=SS[:, 0, t:t + 1]),
                     reads=[XR[t], SSR], writes=[junkr, SSR])
            S.op("act", lambda e: e.activation(out=SS[:, 2, :], in_=SS[:, 0, :], func=AF.Sqrt, scale=1.0 / D, bias=EPS_T[:, 0:1]),
                 reads=[SSR, EPSR], writes=[SSR])
            S.op("dve", lambda e: e.reciprocal(out=SS[:, 1, :], in_=SS[:, 2, :]), reads=[SSR], writes=[SSR])
            for t in range(NT):
                i = t % 2
                S.op("dve", lambda e, t=t, i=i: e.scalar_tensor_tensor(out=xn_buf[i], in0=X[:, t, :], scalar=SS[:, 1, t:t + 1],
                                                                      in1=GB[gk], op0=ALU.mult, op1=ALU.mult),
                     reads=[XR[t], SSR, GBR[gk]], writes=[xn_res[i]])
                b = bank()
                psT = PS[b][:, :].bitcast(BF16).rearrange("p (a b) -> p a b", a=8)
                for c in range(8):
                    S.op("pe", lambda e, c=c, i=i, psT=psT: e.transpose(out=psT[:, c, :], in_=xn_buf[i][:, c * 128:(c + 1) * 128],
                                                                       identity=IDB),
                         reads=[xn_res[i], IDBR], writes=[PSR[b]], signal=(c == 7))
                eng = "act" if t % 2 == 0 else "dve"
                if eng == "act":
                    S.op("act", lambda e, t=t, psT=psT: e.activation(out=XNT[:, :, t * 128:(t + 1) * 128], in_=psT, func=AF.Copy),
                         reads=[PSR[b]], writes=[XNTR[t]])
                else:
                    S.op("dve", lambda e, t=t, psT=psT: e.tensor_copy(out=XNT[:, :, t * 128:(t + 1) * 128], in_=psT),
                         reads=[PSR[b]], writes=[XNTR[t]])

        def inproj_fm(slot, col0, tb, out_bank):
            pairs = [(WB[slot][:, dc, col0:col0 + 128], XNT[:, dc, tb * 512:(tb + 1) * 512]) for dc in range(8)]
            mm_group(out_bank, PS[out_bank][:, :], pairs, [WBR[slot]] + XNTR[4 * tb:4 * tb + 4])

        def inproj_tm(slots, t, evac):
            for h, sl in enumerate(slots):
                b = bank()
                pairs = [(XNT[:, dc, t * 128:(t + 1) * 128], WB[sl][:, dc, :]) for dc in range(8)]
                mm_group(b, PS[b][:, 0:256], pairs, [WBR[sl], XNTR[t]])
                evac(b, h)

        def outproj_partial(gbuf, gres, wok, tiles):
            for j, t in enumerate(tiles):
                for dh in range(2):
                    b = bank()
                    pairs = [(gbuf[:, cc, j * 128:(j + 1) * 128], WO[wok][:, cc, dh * 512:(dh + 1) * 512]) for cc in range(4)]
                    mm_group(b, PS[b][:, :], pairs, list(gres) + [WOR[wok]])
                    S.op("dve", lambda e, t=t, dh=dh, b=b: e.tensor_tensor(out=X[:, t, dh * 512:(dh + 1) * 512],
                                                                           in0=X[:, t, dh * 512:(dh + 1) * 512],
                                                                           in1=PS[b][:, :], op=ALU.add),
                         reads=[PSR[b], XR[t]], writes=[XR[t]])

        EPS_T = carve(2, F32); EPSR = Res("eps")
        S.op("dve", lambda e: e.memset(EPS_T[:, 0:1], 1e-6), writes=[EPSR])
        S.op("dve", lambda e: e.memset(EPS_T[:, 1:2], 1e-5), writes=[EPSR])
        scratch0 = off[0]

        def scratch_reset():
            off[0] = scratch0

        def layer_common_begin(L, gk):
            scratch_reset()
            junk = Gfull[:, 0:D]; junkr = Res("junk")
            xn_buf = [Gfull[:, D:2 * D], Gfull[:, 2 * D:3 * D]]
            xn_res = [Res("xn0"), Res("xn1")]
            rms_stats()
            norm_and_transpose(gk, junk, junkr, xn_buf, xn_res)
            scratch_reset()

        def layer3(L, gk):
            layer_common_begin(L, gk)
            CH = [carve(2080) for _ in range(2)]; CHR = [Res("ch0"), Res("ch1")]
            BZ = [carve(2048) for _ in range(2)]; BZR = [Res("bz0"), Res("bz1")]
            ACC = carve(2048, F32); ACCR = Res("acc")
            SZ = [carve(512) for _ in range(2)]; SZR = [Res("sz0"), Res("sz1")]
            HS = [carve(512) for _ in range(2)]; HSR = [Res("hs0"), Res("hs1")]
            for i in range(2):
                S.op("dve", lambda e, i=i: e.memset(CH[i], 0.0), writes=[CHR[i]])
            G2 = carve(4 * S_LEN).rearrange("p (c s) -> p c s", c=4)
            Gb = [G, G2]
            GRb = [GR, [Res("g2_%d" % c) for c in range(4)]]
            def conv_block(pc, n):
                pci, pcg, pcc, pG, pGR = pc
                c0 = n * 512
                S.op("dve", lambda e: e.tensor_scalar_mul(out=ACC[:, c0:c0 + 512], in0=CH[pci][:, c0:c0 + 512],
                                                         scalar1=VEC[:, pcg, V_SC:V_SC + 1]),
                     reads=[CHR[pci], VECR], writes=[ACCR])
                for j in (1, 2):
                    S.op("dve", lambda e, j=j: e.scalar_tensor_tensor(
                        out=ACC[:, c0:c0 + 512], in0=CH[pci][:, c0 + j:c0 + j + 512], scalar=VEC[:, pcg, V_SC + j:V_SC + j + 1],
                        in1=ACC[:, c0:c0 + 512], op0=ALU.mult, op1=ALU.add), reads=[CHR[pci], VECR, ACCR], writes=[ACCR])
                S.op("dve", lambda e: e.tensor_tensor(out=pG[:, pcc, c0:c0 + 512], in0=ACC[:, c0:c0 + 512],
                                                      in1=BZ[pci][:, c0:c0 + 512], op=ALU.mult),
                     reads=[ACCR, BZR[pci]], writes=[pGR[pcc]])

            it = 0
            pending = None
            conv_todo = None
            for g in range(4):
                Gc, GRc = Gb[g % 2], GRb[g % 2]
                for half in range(2):
                    ch = g * 2 + half
                    sl = [load_w(L, kind * 8 + ch) for kind in range(4)]
                    if g == 0 and half == 0:
                        wok_next = load_wo(L, 0)
                    for c2 in range(2):
                        cc = half * 2 + c2
                        cg = g * 4 + cc
                        ci = cg % 2
                        col0 = c2 * 128
                        for tb in range(4):
                            i2 = it % 2; it += 1
                            bz_, bh_, bc_, bb_ = bank(), bank(), bank(), bank()
                            inproj_fm(sl[3], col0, tb, bz_)
                            inproj_fm(sl[2], col0, tb, bh_)
                            inproj_fm(sl[1], col0, tb, bc_)
                            inproj_fm(sl[0], col0, tb, bb_)
                            S.op("act", lambda e, b=bz_, i2=i2: e.activation(out=SZ[i2], in_=PS[b][:, :], func=AF.Silu),
                                 reads=[PSR[bz_]], writes=[SZR[i2]])
                            S.op("act", lambda e, b=bh_, i2=i2: e.activation(out=HS[i2], in_=PS[b][:, :], func=AF.Copy),
                                 reads=[PSR[bh_]], writes=[HSR[i2]])
                            S.op("dve", lambda e, b=bc_, i2=i2, ci=ci, tb=tb: e.tensor_tensor(
                                out=CH[ci][:, 1 + tb * 512:1 + (tb + 1) * 512], in0=PS[b][:, :], in1=HS[i2], op=ALU.mult),
                                reads=[PSR[bc_], HSR[i2]], writes=[CHR[ci]])
                            S.op("dve", lambda e, b=bb_, i2=i2, ci=ci, tb=tb: e.tensor_tensor(
                                out=BZ[ci][:, tb * 512:(tb + 1) * 512], in0=PS[b][:, :], in1=SZ[i2], op=ALU.mult),
                                reads=[PSR[bb_], SZR[i2]], writes=[BZR[ci]])
                            if conv_todo is not None:
                                conv_block(conv_todo, tb)
                        conv_todo = (ci, cg, cc, Gc, GRc)
                        if pending is not None and cc == 0:
                            outproj_partial(*pending)
                            pending = None
                            wok_next = load_wo(L, g)
                pending = (Gc, GRc, wok_next, list(range(NT)))
            for n in range(4):
                conv_block(conv_todo, n)
            outproj_partial(*pending)

        def mix_tail(L, g, kb, wsl_z, WM, WMR, FT, FTR, SZK, SZKR, wok, bias_row, scale_row):
            for cc in range(4):
                b = bank()
                inproj_fm(wsl_z[cc // 2], (cc % 2) * 128, kb, b)
                S.op("act", lambda e, b=b, cc=cc: e.activation(out=SZK[:, cc, :], in_=PS[b][:, :], func=AF.Silu),
                     reads=[PSR[b]], writes=[SZKR[cc]])
            for dt_ in range(4):
                b = bank()
                pairs = [(WM[:, lc, dt_ * 128:(dt_ + 1) * 128], FT[:, lc, :]) for lc in range(4)]
                mm_group(b, PS[b][:, :], pairs, [WMR] + FTR)
                cg = g * 4 + dt_
                if bias_row is not None:
                    S.op("dve", lambda e, b=b, dt_=dt_, cg=cg: e.scalar_tensor_tensor(
                        out=SZK[:, dt_, :], in0=PS[b][:, :], scalar=VEC[:, cg, bias_row:bias_row + 1], in1=SZK[:, dt_, :],
                        op0=ALU.add, op1=ALU.mult), reads=[PSR[b], VECR, SZKR[dt_]], writes=[SZKR[dt_]])
                else:
                    S.op("dve", lambda e, b=b, dt_=dt_, cg=cg: e.scalar_tensor_tensor(
                        out=SZK[:, dt_, :], in0=PS[b][:, :], scalar=VEC[:, cg, scale_row:scale_row + 1], in1=SZK[:, dt_, :],
                        op0=ALU.mult, op1=ALU.mult), reads=[PSR[b], VECR, SZKR[dt_]], writes=[SZKR[dt_]])
            return (SZK, SZKR, wok, [4 * kb + j for j in range(4)])

        def load_group_mat(dst, dst_res, src, g, key):
            S.dma("pool", lambda e: e.dma_start(out=dst, in_=src[g]), key, writes=[dst_res])

        def layer0(L, gk):
            layer_common_begin(L, gk)
            UG = G.rearrange("p c s -> p (c s)").rearrange("p (t n) -> p t n", t=NT)
            UGR = [Res("ug%d" % t) for t in range(NT)]
            CSC = carve(4 * 1024).rearrange("p (c n) -> p c n", c=4); CSCR = Res("csc")
            WM = carve(4 * 512).rearrange("p (c n) -> p c n", c=4); WMR = Res("wm")
            NCSS = 6
            CSS = [carve(352) for _ in range(NCSS)]; CSSR = [Res("css%d" % i) for i in range(NCSS)]
            PT = [carve(4 * 352).rearrange("p (c n) -> p c n", c=4) for _ in range(2)]
            PTR = [[Res("pt%d_%d" % (a, c)) for c in range(4)] for a in range(2)]
            FTF = carve(4 * S_LEN).rearrange("p (c n) -> p c n", c=4); FTR = [Res("ft%d" % c) for c in range(4)]
            SZKb = [carve(4 * 512).rearrange("p (c n) -> p c n", c=4) for _ in range(2)]
            SZKRb = [[Res("szk%d_%d" % (i, c)) for c in range(4)] for i in range(2)]
            FE = [[carve(8 * 128).rearrange("p (c n) -> p c n", c=8) for _ in range(2)] for _ in range(1)]
            FER = [[Res("fe%d_%d" % (i, j)) for j in range(2)] for i in range(1)]
            BS = [carve(352, F32) for _ in range(2)]; BSR = [Res("bs0"), Res("bs1")]
            U0 = carve(512); U0R = Res("u0")
            ONER = carve(512); ONERR = Res("oner")
            S.op("dve", lambda e: e.memset(ONER, 1.0 / 32.0), writes=[ONERR])
            S.dma("sp", lambda e: e.dma_start(out=CSC.rearrange("p c n -> p (c n)"), in_=csc_d[:, :]), "k_csc", writes=[CSCR])
            css_i = 0
            fe_i = 0
            bs_i = 0
            for g in range(4):
                su = [load_w(L, 2 * g), load_w(L, 2 * g + 1)]
                sz_ = [load_w(L, 8 + 2 * g), load_w(L, 8 + 2 * g + 1)]
                load_group_mat(WM, WMR, wm_d, g, "k_wm")
                wok = load_wo(L, g)
                for h, sl in enumerate(su):
                    b = bank()
                    pairs = [(XNT[:, dc, 0:1], WB[sl][:, dc, :]) for dc in range(8)]
                    mm_group(b, PS[b][0:1, 0:256], pairs, [WBR[sl], XNTR[0]])
                    S.op("act", lambda e, b=b, h=h: e.activation(out=U0[0:1, h * 256:(h + 1) * 256], in_=PS[b][0:1, 0:256], func=AF.Copy),
                         reads=[PSR[b]], writes=[U0R])
                if _STOP == 1:
                    continue
                for tau in range(8):
                    fb = 0
                    lo_ = 1 + 128 * tau
                    hi_ = 128 * (16 - tau) - 1
                    rd = [XNTR[tau], XNTR[min(tau + 1, NT - 1)], XNTR[15 - tau]]
                    S.op("dve", lambda e, fb=fb, lo_=lo_, hi_=hi_: e.tensor_tensor(
                        out=FE[fb][0], in0=XNT[:, :, lo_:lo_ + 128], in1=XNT[:, :, hi_:hi_ - 128:-1], op=ALU.add),
                        reads=rd, writes=[FER[fb][0]])
                    S.op("dve", lambda e, fb=fb, lo_=lo_, hi_=hi_: e.tensor_tensor(
                        out=FE[fb][1], in0=XNT[:, :, lo_:lo_ + 128], in1=XNT[:, :, hi_:hi_ - 128:-1], op=ALU.subtract),
                        reads=rd, writes=[FER[fb][1]])
                    for eo in range(2):
                        t = 8 * eo + tau
                        for h, sl in enumerate(su):
                            b = bank()
                            pairs = [(FE[fb][eo][:, dc, :], WB[sl][:, dc, :]) for dc in range(8)]
                            mm_group(b, PS[b][:, 0:256], pairs, [WBR[sl], FER[fb][eo]])
                            if h == 0:
                                S.op("act", lambda e, b=b, t=t: e.activation(out=UG[:, t, 0:256], in_=PS[b][:, 0:256], func=AF.Copy),
                                     reads=[PSR[b]], writes=[UGR[t]])
                            else:
                                S.op("dve", lambda e, b=b, t=t: e.tensor_copy(out=UG[:, t, 256:512], in_=PS[b][:, 0:256]),
                                     reads=[PSR[b]], writes=[UGR[t]])
                if _STOP == 2:
                    continue
                for kb in range(3):
                    k0, nk = KB0[kb], KBN[kb]
                    for part in range(2):
                        banks = [bank() for _ in range(4)]
                        for tau in range(8):
                            k = css_i % NCSS; css_i += 1
                            S.dma("sp", lambda e, k=k, part=part, kb=kb, tau=tau: e.dma_start(out=CSS[k][:, 0:KB_W], in_=css_d[part, kb, tau]),
                                  "k_css%d" % k, writes=[CSSR[k]])
                            last = (tau == 7) and part == 1
                            for cc in range(4):
                                b = banks[cc]
                                S.op("pe", lambda e, b=b, cc=cc, tau=tau, k=k, part=part, last=last: e.matmul(
                                    PS[b][:, 0:KB_W], UG[:, 8 * part + tau, cc * 128:(cc + 1) * 128], CSS[k][:, 0:KB_W],
                                    start=(tau == 0), stop=last),
                                    reads=[UGR[8 * part + tau], CSSR[k]], writes=[PSR[b]], signal=True)
                        if part == 0:
                            for cc in range(4):
                                b = banks[cc]
                                S.op("pe", lambda e, b=b, cc=cc: e.matmul(
                                    PS[b][:, 0:KB_W], U0[0:1, cc * 128:(cc + 1) * 128], ONER[0:1, 0:KB_W], start=False, stop=True),
                                    reads=[U0R, ONERR], writes=[PSR[b]], signal=True)
                        for cc in range(4):
                            b = banks[cc]
                            if cc % 2 == 0:
                                S.op("act", lambda e, b=b, cc=cc, part=part: e.activation(out=PT[part][:, cc, 0:KB_W], in_=PS[b][:, 0:KB_W], func=AF.Copy),
                                     reads=[PSR[b]], writes=[PTR[part][cc]])
                            else:
                                S.op("dve", lambda e, b=b, cc=cc, part=part: e.tensor_copy(out=PT[part][:, cc, 0:KB_W], in_=PS[b][:, 0:KB_W]),
                                     reads=[PSR[b]], writes=[PTR[part][cc]])
                    if _STOP == 3:
                        continue
                    for lt in range(4):
                        ba, bb_ = bank(), bank()
                        mm_group(ba, PS[ba][:, 0:KB_W], [(CSC[:, cc, lt * 128:(lt + 1) * 128], PT[0][:, cc, 0:KB_W]) for cc in range(4)],
                                 [CSCR] + PTR[0])
                        mm_group(bb_, PS[bb_][:, 0:KB_W], [(CSC[:, cc, 512 + lt * 128:512 + (lt + 1) * 128], PT[1][:, cc, 0:KB_W]) for cc in range(4)],
                                 [CSCR] + PTR[1])
                        bi = bs_i % 2; bs_i += 1
                        S.op("act", lambda e, b=bb_, bi=bi: e.activation(out=BS[bi][:, 0:KB_W], in_=PS[b][:, 0:KB_W], func=AF.Copy),
                             reads=[PSR[bb_]], writes=[BSR[bi]])
                        S.op("dve", lambda e, b=ba, bi=bi, lt=lt, k0=k0, nk=nk: e.tensor_tensor(
                            out=FTF[:, lt, k0:k0 + nk], in0=PS[b][:, 0:nk], in1=BS[bi][:, 0:nk], op=ALU.add),
                            reads=[PSR[ba], BSR[bi]], writes=[FTR[lt]])
                        ka = max(k0, 1)
                        kz = min(k0 + nk, 1024)
                        S.op("dve", lambda e, b=ba, bi=bi, lt=lt, k0=k0, ka=ka, kz=kz: e.tensor_tensor(
                            out=FTF[:, lt, S_LEN - ka:S_LEN - kz:-1], in0=PS[b][:, ka - k0:kz - k0], in1=BS[bi][:, ka - k0:kz - k0],
                            op=ALU.subtract), reads=[PSR[ba], BSR[bi]], writes=[FTR[lt]])
                if _STOP == 4:
                    continue
                pend = None
                for tb in range(4):
                    nxt = mix_tail(L, g, tb, sz_, WM, WMR, FTF[:, :, tb * 512:(tb + 1) * 512], FTR, SZKb[tb % 2], SZKRb[tb % 2], wok, V_BMIX, None)
                    if pend is not None:
                        outproj_partial(*pend)
                    pend = nxt
                outproj_partial(*pend)

        def layer2(L, gk):
            layer_common_begin(L, gk)
            UG = G.rearrange("p c s -> p (c s)").rearrange("p (t n) -> p t n", t=NT)
            UGR = [Res("ug%d" % t) for t in range(NT)]
            PM = carve(4 * 5 * 128).rearrange("p (g k n) -> p g k n", g=4, k=5); PMR = Res("pm")
            WM = carve(4 * 512).rearrange("p (c n) -> p c n", c=4); WMR = Res("wm")
            FT = carve(4 * 512).rearrange("p (c n) -> p c n", c=4); FTR = [Res("ft%d" % c) for c in range(4)]
            SZKb = [carve(4 * 512).rearrange("p (c n) -> p c n", c=4) for _ in range(2)]
            SZKRb = [[Res("szk%d_%d" % (i, c)) for c in range(4)] for i in range(2)]
            pend = None
            S.dma("sp", lambda e: e.dma_start(out=PM.rearrange("p g k n -> p (g k n)"), in_=pm_d[:, :]), "k_pm", writes=[PMR])
            for g in range(4):
                su = [load_w(L, 2 * g), load_w(L, 2 * g + 1)]
                sz_ = [load_w(L, 8 + 2 * g), load_w(L, 8 + 2 * g + 1)]
                load_group_mat(WM, WMR, wg_d, g, "k_wm")
                wok = load_wo(L, g)
                for t in range(NT):
                    def evac(b, h, t=t):
                        if h == 0:
                            S.op("act", lambda e: e.activation(out=UG[:, t, 0:256], in_=PS[b][:, 0:256], func=AF.Copy),
                                 reads=[PSR[b]], writes=[UGR[t]])
                        else:
                            S.op("dve", lambda e: e.tensor_copy(out=UG[:, t, 256:512], in_=PS[b][:, 0:256]),
                                 reads=[PSR[b]], writes=[UGR[t]])
                    inproj_tm(su, t, evac)
                for kb in range(4):
                    for cc in range(4):
                        for j in range(4):
                            T = 4 * kb + j
                            b = bank()
                            pairs = []
                            rd = [PMR]
                            for sc in (T - 1, T, T + 1):
                                if sc < 0 or sc >= NT:
                                    continue
                                if sc == T:
                                    blk = 0 if T == 0 else (2 if T == NT - 1 else 1)
                                elif sc == T - 1:
                                    blk = 3
                                else:
                                    blk = 4
                                pairs.append((UG[:, sc, cc * 128:(cc + 1) * 128], PM[:, g, blk, :]))
                                rd.append(UGR[sc])
                            mm_group(b, PS[b][:, 0:128], pairs, rd)
                            if (cc + j) % 2 == 0:
                                S.op("act", lambda e, b=b, cc=cc, j=j: e.activation(out=FT[:, cc, j * 128:(j + 1) * 128], in_=PS[b][:, 0:128], func=AF.Copy),
                                     reads=[PSR[b]], writes=[FTR[cc]])
                            else:
                                S.op("dve", lambda e, b=b, cc=cc, j=j: e.tensor_copy(out=FT[:, cc, j * 128:(j + 1) * 128], in_=PS[b][:, 0:128]),
                                     reads=[PSR[b]], writes=[FTR[cc]])
                    nxt = mix_tail(L, g, kb, sz_, WM, WMR, FT, FTR, SZKb[kb % 2], SZKRb[kb % 2], wok, None, V_PSC)
                    if pend is not None:
                        outproj_partial(*pend)
                    pend = nxt
                outproj_partial(*pend)
                pend = None

        def layer1(L, gk):
            layer_common_begin(L, gk)
            p1_base = off[0]
            HP = carve(2080); HPR = Res("hp")
            DG = carve(31 * 128).rearrange("p (j n) -> p j n", j=31); DGR = Res("dg")
            AD = [carve(512, F32) for _ in range(2)]; ADR = [Res("ad0"), Res("ad1")]
            APL = [carve(512, F32) for _ in range(2)]; APLR = [Res("ap0"), Res("ap1")]
            TH = carve(512); THR = Res("th")
            SQ = carve(512, F32); SQR = Res("sq")
            assert off[0] - p1_base >= 4 * S_LEN
            p1_end = off[0]
            G2 = big[:, p1_base:p1_base + 4 * S_LEN].rearrange("p (c s) -> p c s", c=4)
            S1 = carve(2048, F32); S1R = [Res("s1_%d" % i) for i in range(4)]
            S2 = carve(2048, F32); S2R = [Res("s2_%d" % i) for i in range(4)]
            HC = [carve(2048) for _ in range(2)]; HCR = [Res("hc0"), Res("hc1")]
            T1 = carve(512, F32); T1R = Res("t1")
            T3 = carve(512); T3R = Res("t3")
            SZ = carve(512); SZR = Res("sz")
            HCD = [Res("hcd%d" % i) for i in range(16)]
            PE_TAPS = list(range(0, 31)); DVE_TAPS = []; POOL_TAPS = []
            tbi = 0
            S.op("dve", lambda e: e.memset(HP, 0.0), writes=[HPR])
            S.op("dve", lambda e: e.memset(S1, 0.0), writes=S1R)
            S.op("dve", lambda e: e.memset(S2, 0.0), writes=S2R)
            idb_b = IDB.unsqueeze(1).to_broadcast([128, 31, 128])
            for ch in range(8):
                sa = load_w(L, ch)
                sg = load_w(L, 8 + ch)
                for c2 in range(2):
                    cg = ch * 2 + c2
                    col0 = c2 * 128
                    hi = cg % 2
                    S.op("dve", lambda e, cg=cg: e.scalar_tensor_tensor(
                        out=DG, in0=idb_b, scalar=0.5, in1=VEC[:, cg, 0:31].unsqueeze(2).to_broadcast([128, 31, 128]),
                        op0=ALU.mult, op1=ALU.mult), reads=[IDBR, VECR], writes=[DGR])
                    for tb in range(4):
                        ba, bg = bank(), bank()
                        inproj_fm(sg, col0, tb, bg)
                        inproj_fm(sa, col0, tb, ba)
                        S.op("act", lambda e, b=bg: e.activation(out=TH, in_=PS[b][:, :], func=AF.Tanh, scale=0.5),
                             reads=[PSR[bg]], writes=[THR])
                        S.op("dve", lambda e, b=ba, tb=tb: e.scalar_tensor_tensor(
                            out=HP[:, 15 + tb * 512:15 + (tb + 1) * 512], in0=TH, scalar=1.0, in1=PS[b][:, :],
                            op0=ALU.add, op1=ALU.mult), reads=[THR, PSR[ba]], writes=[HPR])
                    for tb in range(4):
                        q = tbi % 2; tbi += 1
                        for eng, taps, acc, accr in (("pool", POOL_TAPS, APL[q], APLR[q]), ("dve", DVE_TAPS, AD[q], ADR[q])):
                            for n_, j in enumerate(taps):
                                src = HP[:, tb * 512 + j:tb * 512 + j + 512]
                                if n_ == 0:
                                    S.op(eng, lambda e, src=src, acc=acc, cg=cg, j=j: e.tensor_scalar_mul(out=acc, in0=src, scalar1=VEC[:, cg, j:j + 1]),
                                         reads=[HPR, VECR], writes=[accr])
                                else:
                                    S.op(eng, lambda e, src=src, acc=acc, cg=cg, j=j: e.scalar_tensor_tensor(
                                        out=acc, in0=src, scalar=VEC[:, cg, j:j + 1], in1=acc, op0=ALU.mult, op1=ALU.add),
                                        reads=[HPR, VECR, accr], writes=[accr])
                        b = bank()
                        pairs = [(DG[:, j, :], HP[:, tb * 512 + j:tb * 512 + j + 512]) for j in PE_TAPS]
                        mm_group(b, PS[b][:, :], pairs, [DGR, HPR])
                        S.op("act", lambda e, b=b, tb=tb, hi=hi, cg=cg: e.activation(
                            out=HC[hi][:, tb * 512:(tb + 1) * 512], in_=PS[b][:, :], func=AF.Identity,
                            bias=VEC[:, cg, V_DWB:V_DWB + 1]), reads=[PSR[b], VECR], writes=[HCR[hi]])
                        S.op("act", lambda e, b=b, cg=cg: e.activation(out=SQ, in_=PS[b][:, :], func=AF.Square,
                                                                      bias=VEC[:, cg, V_DWB:V_DWB + 1]),
                             reads=[PSR[b], VECR], writes=[SQR])
                        S.op("dve", lambda e, tb=tb, hi=hi: e.tensor_tensor(out=S1[:, tb * 512:(tb + 1) * 512],
                                                                            in0=S1[:, tb * 512:(tb + 1) * 512],
                                                                            in1=HC[hi][:, tb * 512:(tb + 1) * 512], op=ALU.add),
                             reads=[HCR[hi], S1R[tb]], writes=[S1R[tb]])
                        S.op("dve", lambda e, tb=tb: e.tensor_tensor(out=S2[:, tb * 512:(tb + 1) * 512],
                                                                     in0=S2[:, tb * 512:(tb + 1) * 512], in1=SQ, op=ALU.add),
                             reads=[SQR, S2R[tb]], writes=[S2R[tb]])
                    S.dma("sp", lambda e, hi=hi, cg=cg: e.dma_start(out=hc_d[cg * 128:(cg + 1) * 128, :], in_=HC[hi]),
                          "k_hcst%d" % hi, reads=[HCR[hi]], writes=[HCD[cg]])
            for tb in range(4):
                sl = slice(tb * 512, (tb + 1) * 512)
                b1, b2 = bank(), bank()
                mm_group(b1, PS[b1][:, :], [(ONEF, S1[:, sl])], [ONEFR, S1R[tb]])
                mm_group(b2, PS[b2][:, :], [(ONEF, S2[:, sl])], [ONEFR, S2R[tb]])
                S.op("act", lambda e, b=b1, sl=sl: e.activation(out=S1[:, sl], in_=PS[b][:, :], func=AF.Copy, scale=1.0 / E),
                     reads=[PSR[b1]], writes=[S1R[tb]])
                S.op("dve", lambda e, sl=sl: e.tensor_tensor(out=T1, in0=S1[:, sl], in1=S1[:, sl], op=ALU.mult),
                     reads=[S1R[tb]], writes=[T1R])
                S.op("dve", lambda e, b=b2, sl=sl: e.scalar_tensor_tensor(out=S2[:, sl], in0=PS[b][:, :], scalar=1.0 / E, in1=T1,
                                                                          op0=ALU.mult, op1=ALU.subtract),
                     reads=[PSR[b2], T1R], writes=[S2R[tb]])
                S.op("act", lambda e, sl=sl: e.activation(out=S2[:, sl], in_=S2[:, sl], func=AF.Sqrt, bias=EPS_T[:, 1:2]),
                     reads=[S2R[tb], EPSR], writes=[S2R[tb]])
                S.op("dve", lambda e, sl=sl: e.reciprocal(out=S2[:, sl], in_=S2[:, sl]), reads=[S2R[tb]], writes=[S2R[tb]])
            NB2 = 3
            sp_ = [p1_base + 4 * S_LEN]

            def carve_p1(n, dtype=BF16):
                units = n * (2 if dtype == F32 else 1)
                a = big[:, sp_[0]:sp_[0] + units]
                sp_[0] += units
                assert sp_[0] <= p1_end
                return a.bitcast(F32) if dtype == F32 else a
            T1b = [T1] + [carve_p1(512, F32) for _ in range(NB2 - 1)]; T1bR = [T1R] + [Res("t1b%d" % i) for i in range(NB2 - 1)]
            T3b = [T3] + [carve_p1(512) for _ in range(NB2 - 1)]; T3bR = [T3R] + [Res("t3b%d" % i) for i in range(NB2 - 1)]
            SZb = [SZ] + [carve(512) for _ in range(NB2 - 1)]; SZbR = [SZR] + [Res("szb%d" % i) for i in range(NB2 - 1)]
            S.barrier()
            Gb = [G, G2]
            GRb = [GR, [Res("g2_%d" % c) for c in range(4)]]
            hcv = hc_d
            it = 0
            i2 = 0
            pending = None
            wok_next = load_wo(L, 0)
            for g in range(4):
                Gc, GRc = Gb[g % 2], GRb[g % 2]
                sz_ = [load_w(L, 16 + 2 * g), load_w(L, 16 + 2 * g + 1)]
                for cc in range(4):
                    cg = g * 4 + cc
                    hi = it % 2; it += 1
                    S.dma("sp", lambda e, hi=hi, cg=cg: e.dma_start(out=HC[hi], in_=hcv[cg * 128:(cg + 1) * 128, :]),
                          "k_hcld%d" % hi, reads=[HCD[cg]], writes=[HCR[hi]])
                    for tb in range(4):
                        sl = slice(tb * 512, (tb + 1) * 512)
                        q = i2 % NB2; i2 += 1
                        b = bank()
                        inproj_fm(sz_[cc // 2], (cc % 2) * 128, tb, b)
                        S.op("act", lambda e, b=b, q=q: e.activation(out=SZb[q], in_=PS[b][:, :], func=AF.Silu),
                             reads=[PSR[b]], writes=[SZbR[q]])
                        S.op("pool", lambda e, hi=hi, sl=sl, q=q: e.tensor_tensor(out=T1b[q], in0=HC[hi][:, sl], in1=S1[:, sl], op=ALU.subtract),
                             reads=[HCR[hi], S1R[tb]], writes=[T1bR[q]])
                        S.op("dve", lambda e, sl=sl, q=q: e.tensor_tensor(out=T1b[q], in0=T1b[q], in1=S2[:, sl], op=ALU.mult),
                             reads=[T1bR[q], S2R[tb]], writes=[T1bR[q]])
                        S.op("act", lambda e, cg=cg, q=q: e.activation(out=T3b[q], in_=T1b[q], func=AF.Silu, scale=VEC[:, cg, V_LNG:V_LNG + 1],
                                                                      bias=VEC[:, cg, V_LNB:V_LNB + 1]),
                             reads=[T1bR[q], VECR], writes=[T3bR[q]])
                        S.op("dve", lambda e, cc=cc, sl=sl, q=q, Gc=Gc: e.tensor_tensor(out=Gc[:, cc, sl], in0=T3b[q], in1=SZb[q], op=ALU.mult),
                             reads=[T3bR[q], SZbR[q]], writes=[GRc[cc]])
                    if pending is not None and cc == 0:
                        outproj_partial(*pending)
                        pending = None
                        wok_next = load_wo(L, g)
                pending = (Gc, GRc, wok_next, list(range(NT)))
            outproj_partial(*pending)

        emitters = {0: layer0, 1: layer1, 2: layer2, 3: layer3}
        for L in layers:
            gk = load_gain(L)
            emitters[L](L, gk)
            S.barrier()

        scratch_reset()
        yv = y_d.rearrange("(t p) d -> p t d", p=128)
        if final:
            gk = load_gain(4)
            junk = carve(D); junkr = Res("junk")
            OB = [carve(D, F32) for _ in range(2)]; OBR = [Res("ob0"), Res("ob1")]
            rms_stats()
            for t in range(NT):
                S.op("act", lambda e, t=t: e.activation(out=junk, in_=X[:, t, :], func=AF.Square, accum_out=SS[:, 0, t:t + 1]),
                     reads=[XR[t], SSR], writes=[junkr, SSR])
            S.op("act", lambda e: e.activation(out=SS[:, 2, :], in_=SS[:, 0, :], func=AF.Sqrt, scale=1.0 / D, bias=EPS_T[:, 0:1]),
                 reads=[SSR, EPSR], writes=[SSR])
            S.op("dve", lambda e: e.reciprocal(out=SS[:, 1, :], in_=SS[:, 2, :]), reads=[SSR], writes=[SSR])
            for t in range(NT):
                i = t % 2
                S.op("dve", lambda e, t=t, i=i: e.scalar_tensor_tensor(out=OB[i], in0=X[:, t, :], scalar=SS[:, 1, t:t + 1],
                                                                      in1=GB[gk], op0=ALU.mult, op1=ALU.mult),
                     reads=[XR[t], SSR, GBR[gk]], writes=[OBR[i]])
                S.dma("sp", lambda e, t=t, i=i: e.dma_start(out=yv[:, t, :], in_=OB[i]), "k_y%d" % i, reads=[OBR[i]])
        else:
            for q in range(4):
                S.dma("sp", lambda e, q=q: e.dma_start(out=yv[:, 4 * q:4 * q + 4, :], in_=X[:, 4 * q:4 * q + 4, :]),
                      "k_y%d" % q, reads=XR[4 * q:4 * q + 4])
        S.emit()
    return nc


def _bf(a):
    return np.ascontiguousarray(a.astype(ml_dtypes.bfloat16))


_CONST_CACHE = {}


def _constants():
    if _CONST_CACHE:
        return _CONST_CACHE
    c = {}
    c["idb"] = _bf(np.eye(128, dtype=np.float32))
    c["onef"] = np.ones((128, 128), np.float32)
    cc = np.arange(512)
    ang = 2.0 * np.pi * ((cc[:, None] * cc[None, :]) % 512) / 512.0
    tab = np.concatenate([np.cos(ang), np.sin(ang)], axis=1) / 32.0
    c["csc"] = _bf(tab.reshape(4, 128, 1024).transpose(1, 0, 2).reshape(128, 4096))
    sv = 1 + np.arange(1024)
    kv = np.arange(3 * KB_W)
    ang = 2.0 * np.pi * ((sv[:, None] * kv[None, :]) % S_LEN) / float(S_LEN)
    cosf = np.cos(ang) / 32.0
    sinf = -np.sin(ang) / 32.0
    cosf[1023, :] *= 0.5
    sinf[1023, :] = 0.0
    cosf[:, 1025:] = 0.0
    sinf[:, 1025:] = 0.0
    full = np.stack([cosf, sinf], axis=0)
    c["css"] = _bf(full.reshape(2, 8, 128, 3, KB_W).transpose(0, 3, 1, 2, 4))
    pm = np.zeros((128, 4, 5, 128), np.float64)
    for g, w in enumerate((2, 4, 8, 16)):
        left = w // 2
        right = w - 1 - left
        M = np.zeros((S_LEN, S_LEN), np.float64)
        for t in range(S_LEN):
            lo = max(t - left, 0)
            hi = min(t + right + 1, S_LEN)
            M[lo:hi, t] = 1.0 / (hi - lo)
            M[t, t] -= 1.0
        pm[:, g, 0] = M[0:128, 0:128]
        pm[:, g, 1] = M[128:256, 128:256]
        pm[:, g, 2] = M[S_LEN - 128:, S_LEN - 128:]
        pm[:, g, 3] = M[128:256, 256:384]
        pm[:, g, 4] = M[384:512, 256:384]
    c["pm"] = _bf(pm.reshape(128, 4 * 5 * 128))
    _CONST_CACHE.update(c)
    return c


def _arr_win(w):
    ncol = w.shape[1]
    return np.ascontiguousarray(w.reshape(8, 128, ncol // 256, 256).transpose(2, 1, 0, 3))


def _prep_shared(inp):
    f = lambda k: np.asarray(inp[k], dtype=np.float32)
    sh = dict(_constants())
    sh["ng"] = np.ascontiguousarray(np.concatenate([f("norm_g"), f("final_g")[None, :]], axis=0))
    sh["wi0"] = _arr_win(f("fnet_w_in")[0])
    sh["wi1"] = _arr_win(f("conf_w_in")[0])
    sh["wi2"] = _arr_win(f("pool_w_in")[0])
    sh["wi3"] = _arr_win(f("sc_w_in")[0])
    sh["wo"] = np.ascontiguousarray(f("w_out").reshape(4, 4, 4, 128, 1024).transpose(0, 1, 3, 2, 4))
    sh["wm"] = np.ascontiguousarray(f("fnet_w_mix")[0].reshape(4, 4, 128, 512).transpose(0, 2, 1, 3))
    sh["wg"] = np.ascontiguousarray(f("pool_w_grp")[0].reshape(4, 4, 128, 512).transpose(0, 2, 1, 3))
    rows = np.zeros((NV, E), np.float32)
    rows[V_DW:V_DW + 31] = f("conf_dw_w")[0]
    rows[V_DWB] = f("conf_dw_b")[0]
    rows[V_LNG] = f("conf_ln_g")[0]
    rows[V_LNB] = f("conf_ln_b")[0]
    rows[V_PSC] = f("pool_scale")[0]
    rows[V_BMIX] = f("fnet_b_mix")[0].reshape(E)
    rows[V_SC:V_SC + 3] = f("sc_conv_w")[0]
    sh["vec"] = np.ascontiguousarray(rows.reshape(NV, 16, 128).transpose(2, 1, 0).reshape(128, 16 * NV))
    return sh


_PROG_CACHE = {}
FUSED = True


def _run(layers, final, xs, sh):
    key = (tuple(layers), final)
    if key not in _PROG_CACHE:
        _PROG_CACHE[key] = build_program(list(layers), final)
    nc = _PROG_CACHE[key]
    names = ["ng", "wo", "wm", "wg", "vec", "idb", "onef", "csc", "css", "pm"] + ["wi%d" % L for L in layers]
    in_maps = []
    for b in range(8):
        m = {n: sh[n] for n in names}
        m["x"] = np.ascontiguousarray(xs[b])
        in_maps.append(m)
    res = run_bass_kernel_spmd(nc, in_maps, core_ids=list(range(8)))
    return np.stack([np.asarray(r["y"]) for r in res.results], axis=0)


def kernel(**inputs):
    sh = _prep_shared(inputs)
    x = np.asarray(inputs["x"], dtype=np.float32)
    if FUSED:
        return _run((0, 1, 2, 3), True, x, sh).astype(np.float32)
    cur = x
    for L in range(4):
        cur = _run((L,), L == 3, cur, sh)
    return cur.astype(np.float32)
```

```python
import contextlib
import os
import numpy as np
import ml_dtypes
import concourse.bass as bass
import concourse.mybir as mybir
from concourse.bass_utils import run_bass_kernel_spmd

F32 = mybir.dt.float32
BF16 = mybir.dt.bfloat16
AF = mybir.ActivationFunctionType
ALU = mybir.AluOpType

S_LEN = 2048
D = 1024
E = 2048
NT = 16
TOT = 106000


class Res:
    __slots__ = ("name", "w", "r")

    def __init__(self, name):
        self.name = name
        self.w = None
        self.r = {}


class Sched:
    ENG = ("pe", "act", "dve", "pool", "sp")
    CE = ("pe", "act", "dve", "pool")

    def __init__(self, nc):
        self.nc = nc
        self.ops = {e: [] for e in self.ENG}
        self.cnt = {}
        self.nops = {e: 0 for e in self.CE}
        self.elig = {e: [] for e in self.CE}
        self.waited = {e: {} for e in self.ENG}

    def _deps(self, eng, reads, writes):
        need = {}
        for r in reads:
            if r.w is not None:
                k, v = r.w
                need[k] = max(need.get(k, 0), v)
        for r in writes:
            if r.w is not None:
                k, v = r.w
                need[k] = max(need.get(k, 0), v)
            for k, v in r.r.items():
                need[k] = max(need.get(k, 0), v)
        waits = []
        wd = self.waited[eng]
        for k, v in need.items():
            if eng == "pe" and k == "pe":
                continue
            if wd.get(k, 0) < v:
                wd[k] = v
                waits.append((k, v))
        return waits

    def _mark(self, key, val, reads, writes):
        for r in reads:
            r.r[key] = max(r.r.get(key, 0), val)
        for r in writes:
            r.w = (key, val)
            r.r = {}

    def op(self, eng, fn, reads=(), writes=(), signal=True):
        waits = self._deps(eng, reads, writes)
        self.nops[eng] += 1
        seq = self.nops[eng]
        if signal:
            self.elig[eng].append(seq)
        self.ops[eng].append((waits, fn, (eng, seq)))
        self._mark(eng, seq, reads, writes)
        return seq

    def dma(self, q, fn, key, reads=(), writes=()):
        if key not in self.cnt:
            self.cnt[key] = 0
        waits = self._deps(q, reads, writes)
        self.cnt[key] += 16
        val = self.cnt[key]
        self.ops[q].append((waits, fn, (key, 16)))
        self._mark(key, val, reads, writes)

    def barrier(self):
        for e in self.ENG:
            waits = []
            tgt = dict(self.cnt)
            tgt.update(self.nops)
            for k, v in tgt.items():
                if v > 0 and self.waited[e].get(k, 0) < v and not (e == "pe" and k == "pe"):
                    self.waited[e][k] = v
                    waits.append((k, v))
            if waits:
                self.ops[e].append((waits, None, None))

    def resolve(self):
        import bisect
        needed = {e: set() for e in self.CE}
        for e in self.ENG:
            for waits, fn, inc in self.ops[e]:
                for k, v in waits:
                    if k in needed:
                        el = self.elig[k]
                        i = bisect.bisect_left(el, v)
                        assert i < len(el), ("no signalling op after", k, v)
                        needed[k].add(el[i])
        rank = {e: {q: i + 1 for i, q in enumerate(sorted(needed[e]))} for e in self.CE}
        order = {e: sorted(needed[e]) for e in self.CE}

        def translate(k, v):
            if k not in rank:
                return v
            i = bisect.bisect_left(order[k], v)
            return rank[k][order[k][i]]
        return rank, translate

    def emit(self):
        nc = self.nc
        with contextlib.ExitStack() as st:
            self.barrier()
            rank, translate = self.resolve()
            sems = {k: st.enter_context(nc.semaphore("s_" + k)) for k in list(self.cnt) + list(self.CE)}
            block = st.enter_context(nc.Block())

            def body(e):
                def run(eng):
                    for waits, fn, inc in self.ops[e]:
                        for k, v in waits:
                            eng.wait_ge(sems[k], translate(k, v))
                        if fn is None:
                            continue
                        ins = fn(eng)
                        if inc is None:
                            continue
                        if inc[0] in rank:
                            if inc[1] in rank[inc[0]]:
                                ins.then_inc(sems[inc[0]], 1)
                        else:
                            ins.then_inc(sems[inc[0]], inc[1])
                return run

            block.tensor(body("pe"))
            block.scalar(body("act"))
            block.vector(body("dve"))
            block.gpsimd(body("pool"))
            block.sync(body("sp"))


WIN_COLS = {0: 4096, 1: 6144, 2: 4096, 3: 8192}
V_DW, V_DWB, V_LNG, V_LNB, V_PSC, V_BMIX, V_SC = 0, 31, 32, 33, 34, 35, 36
NV = 40
_STOP = int(os.environ.get('L0_STOP', '9'))
KB_W = 342
KB0 = (0, 342, 684)
KBN = (342, 342, 341)


def build_program(layers, final):
    nc = bass.Bass("TRN2", target_bir_lowering=False)
    dt = nc.dram_tensor
    x_d = dt("x", [S_LEN, D], F32, kind="ExternalInput").ap()
    ng_d = dt("ng", [5, D], F32, kind="ExternalInput").ap()
    wi_d = {L: dt("wi%d" % L, [WIN_COLS[L] // 256, 128, 8, 256], F32, kind="ExternalInput").ap() for L in layers}
    wo_d = dt("wo", [4, 4, 128, 4, 1024], F32, kind="ExternalInput").ap()
    wm_d = dt("wm", [4, 128, 4, 512], F32, kind="ExternalInput").ap()
    wg_d = dt("wg", [4, 128, 4, 512], F32, kind="ExternalInput").ap()
    vec_d = dt("vec", [128, 16 * NV], F32, kind="ExternalInput").ap()
    idb_d = dt("idb", [128, 128], BF16, kind="ExternalInput").ap()
    onef_d = dt("onef", [128, 128], F32, kind="ExternalInput").ap()
    csc_d = dt("csc", [128, 4 * 1024], BF16, kind="ExternalInput").ap()
    css_d = dt("css", [2, 3, 8, 128, KB_W], BF16, kind="ExternalInput").ap()
    pm_d = dt("pm", [128, 4 * 5 * 128], BF16, kind="ExternalInput").ap()
    y_d = dt("y", [S_LEN, D], F32, kind="ExternalOutput").ap()
    hc_d = dt("hc_scr", [E, S_LEN], BF16, kind="Internal").ap() if 1 in layers else None

    with contextlib.ExitStack() as st:
        big = st.enter_context(nc.sbuf_tensor("big", [128, TOT], BF16))
        PS = [st.enter_context(nc.psum_tensor("ps%d" % i, [128, 512], F32)) for i in range(8)]
        PSR = [Res("ps%d" % i) for i in range(8)]
        S = Sched(nc)

        off = [0]

        def carve(n, dtype=BF16):
            units = n * (2 if dtype == F32 else 1)
            a = big[:, off[0]:off[0] + units]
            off[0] += units
            assert off[0] <= TOT, off[0]
            return a.bitcast(F32) if dtype == F32 else a

        X = carve(NT * D, F32).rearrange("p (t d) -> p t d", t=NT)
        XR = [Res("x%d" % t) for t in range(NT)]
        XNT = carve(8 * S_LEN).rearrange("p (c s) -> p c s", c=8)
        XNTR = [Res("xnt%d" % t) for t in range(NT)]
        Gfull = carve(4 * S_LEN)
        G = Gfull.rearrange("p (c s) -> p c s", c=4)
        GR = [Res("g%d" % c) for c in range(4)]
        NWB = 6
        WB = [carve(8 * 256).rearrange("p (c n) -> p c n", c=8) for _ in range(NWB)]
        WBR = [Res("wb%d" % i) for i in range(NWB)]
        WO = [carve(4 * 1024).rearrange("p (c n) -> p c n", c=4) for _ in range(1)]
        WOR = [Res("wo%d" % i) for i in range(1)]
        IDB = carve(128); IDBR = Res("idb")
        ONEF = carve(128, F32); ONEFR = Res("onef")
        VEC = carve(16 * NV, F32).rearrange("p (c r) -> p c r", c=16); VECR = Res("vec")
        GB = [carve(D, F32) for _ in range(1)]
        GBR = [Res("gb%d" % i) for i in range(1)]
        SS = carve(4 * NT, F32).rearrange("p (a t) -> p a t", a=4); SSR = Res("ss")
        scratch0 = off[0]

        wb_rr = [0]

        def load_w(L, chunk):
            k = wb_rr[0] % NWB
            wb_rr[0] += 1
            S.dma("pool", lambda e, k=k: e.dma_start(out=WB[k], in_=wi_d[L][chunk]), "k_wb%d" % k, writes=[WBR[k]])
            return k

        wo_rr = [0]

        def load_wo(L, g):
            k = 0
            S.dma("pool", lambda e, k=k: e.dma_start(out=WO[k], in_=wo_d[L, g]), "k_wo%d" % k, writes=[WOR[k]])
            return k

        bank_rr = [0]

        def bank():
            b = bank_rr[0] % 8
            bank_rr[0] += 1
            return b

        def mm_group(b, out_ap, pairs, reads):
            n = len(pairs)
            for i, (l, r) in enumerate(pairs):
                S.op("pe", lambda e, l=l, r=r, i=i: e.matmul(out_ap, l, r, start=(i == 0), stop=(i == n - 1)),
                     reads=reads, writes=[PSR[b]], signal=(i == n - 1))

        def mm_multi(b, groups, reads):
            tot = sum(len(p) for _, p in groups)
            idx = 0
            for out_ap, pairs in groups:
                n = len(pairs)
                for i, (l, r) in enumerate(pairs):
                    S.op("pe", lambda e, o=out_ap, l=l, r=r, f=(idx == 0), la=(i == n - 1): e.matmul(
                        o, l, r, start=f, stop=la, skip_group_check=True),
                        reads=reads, writes=[PSR[b]], signal=(idx == tot - 1))
                    idx += 1

        S.dma("sp", lambda e: e.dma_start(out=IDB, in_=idb_d[:, :]), "k_c0", writes=[IDBR])
        S.dma("sp", lambda e: e.dma_start(out=ONEF, in_=onef_d[:, :]), "k_c1", writes=[ONEFR])
        S.dma("sp", lambda e: e.dma_start(out=VEC.rearrange("p c r -> p (c r)"), in_=vec_d[:, :]), "k_c2", writes=[VECR])
        xv = x_d.rearrange("(t p) d -> p t d", p=128)
        for q in range(4):
            S.dma("sp", lambda e, q=q: e.dma_start(out=X[:, 4 * q:4 * q + 4, :], in_=xv[:, 4 * q:4 * q + 4, :]),
                  "k_x%d" % q, writes=XR[4 * q:4 * q + 4])
        gb_rr = [0]

        def load_gain(row):
            k = 0
            S.dma("sp", lambda e, k=k: e.dma_start(out=GB[k], in_=ng_d[row:row + 1, :].partition_broadcast(128)),
                  "k_gb%d" % k, writes=[GBR[k]])
            return k

        def rms_stats():
            S.op("dve", lambda e: e.memset(SS[:, 0, :], 0.0), writes=[SSR])

        def norm_and_transpose(gk, junk, junkr, xn_buf, xn_res):
            for t in range(NT):
                S.op("act", lambda e, t=t: e.activation(out=junk, in_=X[:, t, :], func=AF.Square,
                                                        accum_out=SS[:, 0, t:t + 1]),
                     reads=[XR[t], SSR], writes=[junkr, SSR])
            S.op("act", lambda e: e.activation(out=SS[:, 2, :], in_=SS[:, 0, :], func=AF.Sqrt, scale=1.0 / D, bias=EPS_T[:, 0:1]),
                 reads=[SSR, EPSR], writes=[SSR])
            S.op("dve", lambda e: e.reciprocal(out=SS[:, 1, :], in_=SS[:, 2, :]), reads=[SSR], writes=[SSR])
            for t in range(NT):
                i = t % 2
                S.op("dve", lambda e, t=t, i=i: e.scalar_tensor_tensor(out=xn_buf[i], in0=X[:, t, :], scalar=SS[:, 1, t:t + 1],
                                                                      in1=GB[gk], op0=ALU.mult, op1=ALU.mult),
                     reads=[XR[t], SSR, GBR[gk]], writes=[xn_res[i]])
                b = bank()
                psT = PS[b][:, :].bitcast(BF16).rearrange("p (a b) -> p a b", a=8)
                for c in range(8):
                    S.op("pe", lambda e, c=c, i=i, psT=psT: e.transpose(out=psT[:, c, :], in_=xn_buf[i][:, c * 128:(c + 1) * 128],
                                                                       identity=IDB),
                         reads=[xn_res[i], IDBR], writes=[PSR[b]], signal=(c == 7))
                eng = "act" if t % 2 == 0 else "dve"
                if eng == "act":
                    S.op("act", lambda e, t=t, psT=psT: e.activation(out=XNT[:, :, t * 128:(t + 1) * 128], in_=psT, func=AF.Copy),
                         reads=[PSR[b]], writes=[XNTR[t]])
                else:
                    S.op("dve", lambda e, t=t, psT=psT: e.tensor_copy(out=XNT[:, :, t * 128:(t + 1) * 128], in_=psT),
                         reads=[PSR[b]], writes=[XNTR[t]])

        def inproj_fm(slot, col0, tb, out_bank):
            pairs = [(WB[slot][:, dc, col0:col0 + 128], XNT[:, dc, tb * 512:(tb + 1) * 512]) for dc in range(8)]
            mm_group(out_bank, PS[out_bank][:, :], pairs, [WBR[slot]] + XNTR[4 * tb:4 * tb + 4])

        def inproj_tm(slots, t, evac):
            for h, sl in enumerate(slots):
                b = bank()
                pairs = [(XNT[:, dc, t * 128:(t + 1) * 128], WB[sl][:, dc, :]) for dc in range(8)]
                mm_group(b, PS[b][:, 0:256], pairs, [WBR[sl], XNTR[t]])
                evac(b, h)

        def outproj_partial(gbuf, gres, wok, tiles):
            for j, t in enumerate(tiles):
                for dh in range(2):
                    b = bank()
                    pairs = [(gbuf[:, cc, j * 128:(j + 1) * 128], WO[wok][:, cc, dh * 512:(dh + 1) * 512]) for cc in range(4)]
                    mm_group(b, PS[b][:, :], pairs, list(gres) + [WOR[wok]])
                    S.op("dve", lambda e, t=t, dh=dh, b=b: e.tensor_tensor(out=X[:, t, dh * 512:(dh + 1) * 512],
                                                                           in0=X[:, t, dh * 512:(dh + 1) * 512],
                                                                           in1=PS[b][:, :], op=ALU.add),
                         reads=[PSR[b], XR[t]], writes=[XR[t]])

        EPS_T = carve(2, F32); EPSR = Res("eps")
        S.op("dve", lambda e: e.memset(EPS_T[:, 0:1], 1e-6), writes=[EPSR])
        S.op("dve", lambda e: e.memset(EPS_T[:, 1:2], 1e-5), writes=[EPSR])
        scratch0 = off[0]

        def scratch_reset():
            off[0] = scratch0

        def layer_common_begin(L, gk):
            scratch_reset()
            junk = Gfull[:, 0:D]; junkr = Res("junk")
            xn_buf = [Gfull[:, D:2 * D], Gfull[:, 2 * D:3 * D]]
            xn_res = [Res("xn0"), Res("xn1")]
            rms_stats()
            norm_and_transpose(gk, junk, junkr, xn_buf, xn_res)
            scratch_reset()

        def layer3(L, gk):
            layer_common_begin(L, gk)
            CH = [carve(2080) for _ in range(2)]; CHR = [Res("ch0"), Res("ch1")]
            BZ = [carve(2048) for _ in range(2)]; BZR = [Res("bz0"), Res("bz1")]
            ACC = carve(2048, F32); ACCR = Res("acc")
            SZ = [carve(512) for _ in range(2)]; SZR = [Res("sz0"), Res("sz1")]
            HS = [carve(512) for _ in range(2)]; HSR = [Res("hs0"), Res("hs1")]
            for i in range(2):
                S.op("dve", lambda e, i=i: e.memset(CH[i], 0.0), writes=[CHR[i]])
            G2 = carve(4 * S_LEN).rearrange("p (c s) -> p c s", c=4)
            Gb = [G, G2]
            GRb = [GR, [Res("g2_%d" % c) for c in range(4)]]
            def conv_block(pc, n):
                pci, pcg, pcc, pG, pGR = pc
                c0 = n * 512
                S.op("dve", lambda e: e.tensor_scalar_mul(out=ACC[:, c0:c0 + 512], in0=CH[pci][:, c0:c0 + 512],
                                                         scalar1=VEC[:, pcg, V_SC:V_SC + 1]),
                     reads=[CHR[pci], VECR], writes=[ACCR])
                for j in (1, 2):
                    S.op("dve", lambda e, j=j: e.scalar_tensor_tensor(
                        out=ACC[:, c0:c0 + 512], in0=CH[pci][:, c0 + j:c0 + j + 512], scalar=VEC[:, pcg, V_SC + j:V_SC + j + 1],
                        in1=ACC[:, c0:c0 + 512], op0=ALU.mult, op1=ALU.add), reads=[CHR[pci], VECR, ACCR], writes=[ACCR])
                S.op("dve", lambda e: e.tensor_tensor(out=pG[:, pcc, c0:c0 + 512], in0=ACC[:, c0:c0 + 512],
                                                      in1=BZ[pci][:, c0:c0 + 512], op=ALU.mult),
                     reads=[ACCR, BZR[pci]], writes=[pGR[pcc]])

            it = 0
            pending = None
            conv_todo = None
            for g in range(4):
                Gc, GRc = Gb[g % 2], GRb[g % 2]
                for half in range(2):
                    ch = g * 2 + half
                    sl = [load_w(L, kind * 8 + ch) for kind in range(4)]
                    if g == 0 and half == 0:
                        wok_next = load_wo(L, 0)
                    for c2 in range(2):
                        cc = half * 2 + c2
                        cg = g * 4 + cc
                        ci = cg % 2
                        col0 = c2 * 128
                        for tb in range(4):
                            i2 = it % 2; it += 1
                            bz_, bh_, bc_, bb_ = bank(), bank(), bank(), bank()
                            inproj_fm(sl[3], col0, tb, bz_)
                            inproj_fm(sl[2], col0, tb, bh_)
                            inproj_fm(sl[1], col0, tb, bc_)
                            inproj_fm(sl[0], col0, tb, bb_)
                            S.op("act", lambda e, b=bz_, i2=i2: e.activation(out=SZ[i2], in_=PS[b][:, :], func=AF.Silu),
                                 reads=[PSR[bz_]], writes=[SZR[i2]])
                            S.op("act", lambda e, b=bh_, i2=i2: e.activation(out=HS[i2], in_=PS[b][:, :], func=AF.Copy),
                                 reads=[PSR[bh_]], writes=[HSR[i2]])
                            S.op("dve", lambda e, b=bc_, i2=i2, ci=ci, tb=tb: e.tensor_tensor(
                                out=CH[ci][:, 1 + tb * 512:1 + (tb + 1) * 512], in0=PS[b][:, :], in1=HS[i2], op=ALU.mult),
                                reads=[PSR[bc_], HSR[i2]], writes=[CHR[ci]])
                            S.op("dve", lambda e, b=bb_, i2=i2, ci=ci, tb=tb: e.tensor_tensor(
                                out=BZ[ci][:, tb * 512:(tb + 1) * 512], in0=PS[b][:, :], in1=SZ[i2], op=ALU.mult),
                                reads=[PSR[bb_], SZR[i2]], writes=[BZR[ci]])
                            if conv_todo is not None:
                                conv_block(conv_todo, tb)
                        conv_todo = (ci, cg, cc, Gc, GRc)
                        if pending is not None and cc == 0:
                            outproj_partial(*pending)
                            pending = None
                            wok_next = load_wo(L, g)
                pending = (Gc, GRc, wok_next, list(range(NT)))
            for n in range(4):
                conv_block(conv_todo, n)
            outproj_partial(*pending)

        def mix_tail(L, g, kb, wsl_z, WM, WMR, FT, FTR, SZK, SZKR, wok, bias_row, scale_row):
            for cc in range(4):
                b = bank()
                inproj_fm(wsl_z[cc // 2], (cc % 2) * 128, kb, b)
                S.op("act", lambda e, b=b, cc=cc: e.activation(out=SZK[:, cc, :], in_=PS[b][:, :], func=AF.Silu),
                     reads=[PSR[b]], writes=[SZKR[cc]])
            for dt_ in range(4):
                b = bank()
                pairs = [(WM[:, lc, dt_ * 128:(dt_ + 1) * 128], FT[:, lc, :]) for lc in range(4)]
                mm_group(b, PS[b][:, :], pairs, [WMR] + FTR)
                cg = g * 4 + dt_
                if bias_row is not None:
                    S.op("dve", lambda e, b=b, dt_=dt_, cg=cg: e.scalar_tensor_tensor(
                        out=SZK[:, dt_, :], in0=PS[b][:, :], scalar=VEC[:, cg, bias_row:bias_row + 1], in1=SZK[:, dt_, :],
                        op0=ALU.add, op1=ALU.mult), reads=[PSR[b], VECR, SZKR[dt_]], writes=[SZKR[dt_]])
                else:
                    S.op("dve", lambda e, b=b, dt_=dt_, cg=cg: e.scalar_tensor_tensor(
                        out=SZK[:, dt_, :], in0=PS[b][:, :], scalar=VEC[:, cg, scale_row:scale_row + 1], in1=SZK[:, dt_, :],
                        op0=ALU.mult, op1=ALU.mult), reads=[PSR[b], VECR, SZKR[dt_]], writes=[SZKR[dt_]])
            return (SZK, SZKR, wok, [4 * kb + j for j in range(4)])

        def load_group_mat(dst, dst_res, src, g, key):
            S.dma("pool", lambda e: e.dma_start(out=dst, in_=src[g]), key, writes=[dst_res])

        def layer0(L, gk):
            layer_common_begin(L, gk)
            UG = G.rearrange("p c s -> p (c s)").rearrange("p (t n) -> p t n", t=NT)
            UGR = [Res("ug%d" % t) for t in range(NT)]
            CSC = carve(4 * 1024).rearrange("p (c n) -> p c n", c=4); CSCR = Res("csc")
            WM = carve(4 * 512).rearrange("p (c n) -> p c n", c=4); WMR = Res("wm")
            NCSS = 6
            CSS = [carve(352) for _ in range(NCSS)]; CSSR = [Res("css%d" % i) for i in range(NCSS)]
            PT = [carve(4 * 352).rearrange("p (c n) -> p c n", c=4) for _ in range(2)]
            PTR = [[Res("pt%d_%d" % (a, c)) for c in range(4)] for a in range(2)]
            FTF = carve(4 * S_LEN).rearrange("p (c n) -> p c n", c=4); FTR = [Res("ft%d" % c) for c in range(4)]
            SZKb = [carve(4 * 512).rearrange("p (c n) -> p c n", c=4) for _ in range(2)]
            SZKRb = [[Res("szk%d_%d" % (i, c)) for c in range(4)] for i in range(2)]
            FE = [[carve(8 * 128).rearrange("p (c n) -> p c n", c=8) for _ in range(2)] for _ in range(1)]
            FER = [[Res("fe%d_%d" % (i, j)) for j in range(2)] for i in range(1)]
            BS = [carve(352, F32) for _ in range(2)]; BSR = [Res("bs0"), Res("bs1")]
            U0 = carve(512); U0R = Res("u0")
            ONER = carve(512); ONERR = Res("oner")
            S.op("dve", lambda e: e.memset(ONER, 1.0 / 32.0), writes=[ONERR])
            S.dma("sp", lambda e: e.dma_start(out=CSC.rearrange("p c n -> p (c n)"), in_=csc_d[:, :]), "k_csc", writes=[CSCR])
            css_i = 0
            fe_i = 0
            bs_i = 0
            for g in range(4):
                su = [load_w(L, 2 * g), load_w(L, 2 * g + 1)]
                sz_ = [load_w(L, 8 + 2 * g), load_w(L, 8 + 2 * g + 1)]
                load_group_mat(WM, WMR, wm_d, g, "k_wm")
                wok = load_wo(L, g)
                for h, sl in enumerate(su):
                    b = bank()
                    pairs = [(XNT[:, dc, 0:1], WB[sl][:, dc, :]) for dc in range(8)]
                    mm_group(b, PS[b][0:1, 0:256], pairs, [WBR[sl], XNTR[0]])
                    S.op("act", lambda e, b=b, h=h: e.activation(out=U0[0:1, h * 256:(h + 1) * 256], in_=PS[b][0:1, 0:256], func=AF.Copy),
                         reads=[PSR[b]], writes=[U0R])
                if _STOP == 1:
                    continue
                for tau in range(8):
                    fb = 0
                    lo_ = 1 + 128 * tau
                    hi_ = 128 * (16 - tau) - 1
                    rd = [XNTR[tau], XNTR[min(tau + 1, NT - 1)], XNTR[15 - tau]]
                    S.op("dve", lambda e, fb=fb, lo_=lo_, hi_=hi_: e.tensor_tensor(
                        out=FE[fb][0], in0=XNT[:, :, lo_:lo_ + 128], in1=XNT[:, :, hi_:hi_ - 128:-1], op=ALU.add),
                        reads=rd, writes=[FER[fb][0]])
                    S.op("dve", lambda e, fb=fb, lo_=lo_, hi_=hi_: e.tensor_tensor(
                        out=FE[fb][1], in0=XNT[:, :, lo_:lo_ + 128], in1=XNT[:, :, hi_:hi_ - 128:-1], op=ALU.subtract),
                        reads=rd, writes=[FER[fb][1]])
                    for eo in range(2):
                        t = 8 * eo + tau
                        for h, sl in enumerate(su):
                            b = bank()
                            pairs = [(FE[fb][eo][:, dc, :], WB[sl][:, dc, :]) for dc in range(8)]
                            mm_group(b, PS[b][:, 0:256], pairs, [WBR[sl], FER[fb][eo]])
                            if h == 0:
                                S.op("act", lambda e, b=b, t=t: e.activation(out=UG[:, t, 0:256], in_=PS[b][:, 0:256], func=AF.Copy),
                                     reads=[PSR[b]], writes=[UGR[t]])
                            else:
                                S.op("dve", lambda e, b=b, t=t: e.tensor_copy(out=UG[:, t, 256:512], in_=PS[b][:, 0:256]),
                                     reads=[PSR[b]], writes=[UGR[t]])
                if _STOP == 2:
                    continue
                for kb in range(3):
                    k0, nk = KB0[kb], KBN[kb]
                    for part in range(2):
                        banks = [bank() for _ in range(4)]
                        for tau in range(8):
                            k = css_i % NCSS; css_i += 1
                            S.dma("sp", lambda e, k=k, part=part, kb=kb, tau=tau: e.dma_start(out=CSS[k][:, 0:KB_W], in_=css_d[part, kb, tau]),
                                  "k_css%d" % k, writes=[CSSR[k]])
                            last = (tau == 7) and part == 1
                            for cc in range(4):
                                b = banks[cc]
                                S.op("pe", lambda e, b=b, cc=cc, tau=tau, k=k, part=part, last=last: e.matmul(
                                    PS[b][:, 0:KB_W], UG[:, 8 * part + tau, cc * 128:(cc + 1) * 128], CSS[k][:, 0:KB_W],
                                    start=(tau == 0), stop=last),
                                    reads=[UGR[8 * part + tau], CSSR[k]], writes=[PSR[b]], signal=True)
                        if part == 0:
                            for cc in range(4):
                                b = banks[cc]
                                S.op("pe", lambda e, b=b, cc=cc: e.matmul(
                                    PS[b][:, 0:KB_W], U0[0:1, cc * 128:(cc + 1) * 128], ONER[0:1, 0:KB_W], start=False, stop=True),
                                    reads=[U0R, ONERR], writes=[PSR[b]], signal=True)
                        for cc in range(4):
                            b = banks[cc]
                            if cc % 2 == 0:
                                S.op("act", lambda e, b=b, cc=cc, part=part: e.activation(out=PT[part][:, cc, 0:KB_W], in_=PS[b][:, 0:KB_W], func=AF.Copy),
                                     reads=[PSR[b]], writes=[PTR[part][cc]])
                            else:
                                S.op("dve", lambda e, b=b, cc=cc, part=part: e.tensor_copy(out=PT[part][:, cc, 0:KB_W], in_=PS[b][:, 0:KB_W]),
                                     reads=[PSR[b]], writes=[PTR[part][cc]])
                    if _STOP == 3:
                        continue
                    for lt in range(4):
                        ba, bb_ = bank(), bank()
                        mm_group(ba, PS[ba][:, 0:KB_W], [(CSC[:, cc, lt * 128:(lt + 1) * 128], PT[0][:, cc, 0:KB_W]) for cc in range(4)],
                                 [CSCR] + PTR[0])
                        mm_group(bb_, PS[bb_][:, 0:KB_W], [(CSC[:, cc, 512 + lt * 128:512 + (lt + 1) * 128], PT[1][:, cc, 0:KB_W]) for cc in range(4)],
                                 [CSCR] + PTR[1])
                        bi = bs_i % 2; bs_i += 1
                        S.op("act", lambda e, b=bb_, bi=bi: e.activation(out=BS[bi][:, 0:KB_W], in_=PS[b][:, 0:KB_W], func=AF.Copy),
                             reads=[PSR[bb_]], writes=[BSR[bi]])
                        S.op("dve", lambda e, b=ba, bi=bi, lt=lt, k0=k0, nk=nk: e.tensor_tensor(
                            out=FTF[:, lt, k0:k0 + nk], in0=PS[b][:, 0:nk], in1=BS[bi][:, 0:nk], op=ALU.add),
                            reads=[PSR[ba], BSR[bi]], writes=[FTR[lt]])
                        ka = max(k0, 1)
                        kz = min(k0 + nk, 1024)
                        S.op("dve", lambda e, b=ba, bi=bi, lt=lt, k0=k0, ka=ka, kz=kz: e.tensor_tensor(
                            out=FTF[:, lt, S_LEN - ka:S_LEN - kz:-1], in0=PS[b][:, ka - k0:kz - k0], in1=BS[bi][:, ka - k0:kz - k0],
                            op=ALU.subtract), reads=[PSR[ba], BSR[bi]], writes=[FTR[lt]])
                if _STOP == 4:
                    continue
                pend = None
                for tb in range(4):
                    nxt = mix_tail(L, g, tb, sz_, WM, WMR, FTF[:, :, tb * 512:(tb + 1) * 512], FTR, SZKb[tb % 2], SZKRb[tb % 2], wok, V_BMIX, None)
                    if pend is not None:
                        outproj_partial(*pend)
                    pend = nxt
                outproj_partial(*pend)

        def layer2(L, gk):
            layer_common_begin(L, gk)
            UG = G.rearrange("p c s -> p (c s)").rearrange("p (t n) -> p t n", t=NT)
            UGR = [Res("ug%d" % t) for t in range(NT)]
            PM = carve(4 * 5 * 128).rearrange("p (g k n) -> p g k n", g=4, k=5); PMR = Res("pm")
            WM = carve(4 * 512).rearrange("p (c n) -> p c n", c=4); WMR = Res("wm")
            FT = carve(4 * 512).rearrange("p (c n) -> p c n", c=4); FTR = [Res("ft%d" % c) for c in range(4)]
            SZKb = [carve(4 * 512).rearrange("p (c n) -> p c n", c=4) for _ in range(2)]
            SZKRb = [[Res("szk%d_%d" % (i, c)) for c in range(4)] for i in range(2)]
            pend = None
            S.dma("sp", lambda e: e.dma_start(out=PM.rearrange("p g k n -> p (g k n)"), in_=pm_d[:, :]), "k_pm", writes=[PMR])
            for g in range(4):
                su = [load_w(L, 2 * g), load_w(L, 2 * g + 1)]
                sz_ = [load_w(L, 8 + 2 * g), load_w(L, 8 + 2 * g + 1)]
                load_group_mat(WM, WMR, wg_d, g, "k_wm")
                wok = load_wo(L, g)
                for t in range(NT):
                    def evac(b, h, t=t):
                        if h == 0:
                            S.op("act", lambda e: e.activation(out=UG[:, t, 0:256], in_=PS[b][:, 0:256], func=AF.Copy),
                                 reads=[PSR[b]], writes=[UGR[t]])
                        else:
                            S.op("dve", lambda e: e.tensor_copy(out=UG[:, t, 256:512], in_=PS[b][:, 0:256]),
                                 reads=[PSR[b]], writes=[UGR[t]])
                    inproj_tm(su, t, evac)
                for kb in range(4):
                    for cc in range(4):
                        for j in range(4):
                            T = 4 * kb + j
                            b = bank()
                            pairs = []
                            rd = [PMR]
                            for sc in (T - 1, T, T + 1):
                                if sc < 0 or sc >= NT:
                                    continue
                                if sc == T:
                                    blk = 0 if T == 0 else (2 if T == NT - 1 else 1)
                                elif sc == T - 1:
                                    blk = 3
                                else:
                                    blk = 4
                                pairs.append((UG[:, sc, cc * 128:(cc + 1) * 128], PM[:, g, blk, :]))
                                rd.append(UGR[sc])
                            mm_group(b, PS[b][:, 0:128], pairs, rd)
                            if (cc + j) % 2 == 0:
                                S.op("act", lambda e, b=b, cc=cc, j=j: e.activation(out=FT[:, cc, j * 128:(j + 1) * 128], in_=PS[b][:, 0:128], func=AF.Copy),
                                     reads=[PSR[b]], writes=[FTR[cc]])
                            else:
                                S.op("dve", lambda e, b=b, cc=cc, j=j: e.tensor_copy(out=FT[:, cc, j * 128:(j + 1) * 128], in_=PS[b][:, 0:128]),
                                     reads=[PSR[b]], writes=[FTR[cc]])
                    nxt = mix_tail(L, g, kb, sz_, WM, WMR, FT, FTR, SZKb[kb % 2], SZKRb[kb % 2], wok, None, V_PSC)
                    if pend is not None:
                        outproj_partial(*pend)
                    pend = nxt
                outproj_partial(*pend)
                pend = None

        def layer1(L, gk):
            layer_common_begin(L, gk)
            p1_base = off[0]
            HP = carve(2080); HPR = Res("hp")
            DG = carve(31 * 128).rearrange("p (j n) -> p j n", j=31); DGR = Res("dg")
            AD = [carve(512, F32) for _ in range(2)]; ADR = [Res("ad0"), Res("ad1")]
            APL = [carve(512, F32) for _ in range(2)]; APLR = [Res("ap0"), Res("ap1")]
            TH = carve(512); THR = Res("th")
            SQ = carve(512, F32); SQR = Res("sq")
            assert off[0] - p1_base >= 4 * S_LEN
            p1_end = off[0]
            G2 = big[:, p1_base:p1_base + 4 * S_LEN].rearrange("p (c s) -> p c s", c=4)
            S1 = carve(2048, F32); S1R = [Res("s1_%d" % i) for i in range(4)]
            S2 = carve(2048, F32); S2R = [Res("s2_%d" % i) for i in range(4)]
            HC = [carve(2048) for _ in range(2)]; HCR = [Res("hc0"), Res("hc1")]
            T1 = carve(512, F32); T1R = Res("t1")
            T3 = carve(512); T3R = Res("t3")
            SZ = carve(512); SZR = Res("sz")
            HCD = [Res("hcd%d" % i) for i in range(16)]
            PE_TAPS = list(range(0, 31)); DVE_TAPS = []; POOL_TAPS = []
            tbi = 0
            S.op("dve", lambda e: e.memset(HP, 0.0), writes=[HPR])
            S.op("dve", lambda e: e.memset(S1, 0.0), writes=S1R)
            S.op("dve", lambda e: e.memset(S2, 0.0), writes=S2R)
            idb_b = IDB.unsqueeze(1).to_broadcast([128, 31, 128])
            for ch in range(8):
                sa = load_w(L, ch)
                sg = load_w(L, 8 + ch)
                for c2 in range(2):
                    cg = ch * 2 + c2
                    col0 = c2 * 128
                    hi = cg % 2
                    S.op("dve", lambda e, cg=cg: e.scalar_tensor_tensor(
                        out=DG, in0=idb_b, scalar=0.5, in1=VEC[:, cg, 0:31].unsqueeze(2).to_broadcast([128, 31, 128]),
                        op0=ALU.mult, op1=ALU.mult), reads=[IDBR, VECR], writes=[DGR])
                    for tb in range(4):
                        ba, bg = bank(), bank()
                        inproj_fm(sg, col0, tb, bg)
                        inproj_fm(sa, col0, tb, ba)
                        S.op("act", lambda e, b=bg: e.activation(out=TH, in_=PS[b][:, :], func=AF.Tanh, scale=0.5),
                             reads=[PSR[bg]], writes=[THR])
                        S.op("dve", lambda e, b=ba, tb=tb: e.scalar_tensor_tensor(
                            out=HP[:, 15 + tb * 512:15 + (tb + 1) * 512], in0=TH, scalar=1.0, in1=PS[b][:, :],
                            op0=ALU.add, op1=ALU.mult), reads=[THR, PSR[ba]], writes=[HPR])
                    for tb in range(4):
                        q = tbi % 2; tbi += 1
                        for eng, taps, acc, accr in (("pool", POOL_TAPS, APL[q], APLR[q]), ("dve", DVE_TAPS, AD[q], ADR[q])):
                            for n_, j in enumerate(taps):
                                src = HP[:, tb * 512 + j:tb * 512 + j + 512]
                                if n_ == 0:
                                    S.op(eng, lambda e, src=src, acc=acc, cg=cg, j=j: e.tensor_scalar_mul(out=acc, in0=src, scalar1=VEC[:, cg, j:j + 1]),
                                         reads=[HPR, VECR], writes=[accr])
                                else:
                                    S.op(eng, lambda e, src=src, acc=acc, cg=cg, j=j: e.scalar_tensor_tensor(
                                        out=acc, in0=src, scalar=VEC[:, cg, j:j + 1], in1=acc, op0=ALU.mult, op1=ALU.add),
                                        reads=[HPR, VECR, accr], writes=[accr])
                        b = bank()
                        pairs = [(DG[:, j, :], HP[:, tb * 512 + j:tb * 512 + j + 512]) for j in PE_TAPS]
                        mm_group(b, PS[b][:, :], pairs, [DGR, HPR])
                        S.op("act", lambda e, b=b, tb=tb, hi=hi, cg=cg: e.activation(
                            out=HC[hi][:, tb * 512:(tb + 1) * 512], in_=PS[b][:, :], func=AF.Identity,
                            bias=VEC[:, cg, V_DWB:V_DWB + 1]), reads=[PSR[b], VECR], writes=[HCR[hi]])
                        S.op("act", lambda e, b=b, cg=cg: e.activation(out=SQ, in_=PS[b][:, :], func=AF.Square,
                                                                      bias=VEC[:, cg, V_DWB:V_DWB + 1]),
                             reads=[PSR[b], VECR], writes=[SQR])
                        S.op("dve", lambda e, tb=tb, hi=hi: e.tensor_tensor(out=S1[:, tb * 512:(tb + 1) * 512],
                                                                            in0=S1[:, tb * 512:(tb + 1) * 512],
                                                                            in1=HC[hi][:, tb * 512:(tb + 1) * 512], op=ALU.add),
                             reads=[HCR[hi], S1R[tb]], writes=[S1R[tb]])
                        S.op("dve", lambda e, tb=tb: e.tensor_tensor(out=S2[:, tb * 512:(tb + 1) * 512],
                                                                     in0=S2[:, tb * 512:(tb + 1) * 512], in1=SQ, op=ALU.add),
                             reads=[SQR, S2R[tb]], writes=[S2R[tb]])
                    S.dma("sp", lambda e, hi=hi, cg=cg: e.dma_start(out=hc_d[cg * 128:(cg + 1) * 128, :], in_=HC[hi]),
                          "k_hcst%d" % hi, reads=[HCR[hi]], writes=[HCD[cg]])
            for tb in range(4):
                sl = slice(tb * 512, (tb + 1) * 512)
                b1, b2 = bank(), bank()
                mm_group(b1, PS[b1][:, :], [(ONEF, S1[:, sl])], [ONEFR, S1R[tb]])
                mm_group(b2, PS[b2][:, :], [(ONEF, S2[:, sl])], [ONEFR, S2R[tb]])
                S.op("act", lambda e, b=b1, sl=sl: e.activation(out=S1[:, sl], in_=PS[b][:, :], func=AF.Copy, scale=1.0 / E),
                     reads=[PSR[b1]], writes=[S1R[tb]])
                S.op("dve", lambda e, sl=sl: e.tensor_tensor(out=T1, in0=S1[:, sl], in1=S1[:, sl], op=ALU.mult),
                     reads=[S1R[tb]], writes=[T1R])
                S.op("dve", lambda e, b=b2, sl=sl: e.scalar_tensor_tensor(out=S2[:, sl], in0=PS[b][:, :], scalar=1.0 / E, in1=T1,
                                                                          op0=ALU.mult, op1=ALU.subtract),
                     reads=[PSR[b2], T1R], writes=[S2R[tb]])
                S.op("act", lambda e, sl=sl: e.activation(out=S2[:, sl], in_=S2[:, sl], func=AF.Sqrt, bias=EPS_T[:, 1:2]),
                     reads=[S2R[tb], EPSR], writes=[S2R[tb]])
                S.op("dve", lambda e, sl=sl: e.reciprocal(out=S2[:, sl], in_=S2[:, sl]), reads=[S2R[tb]], writes=[S2R[tb]])
            NB2 = 3
            sp_ = [p1_base + 4 * S_LEN]

            def carve_p1(n, dtype=BF16):
                units = n * (2 if dtype == F32 else 1)
                a = big[:, sp_[0]:sp_[0] + units]
                sp_[0] += units
                assert sp_[0] <= p1_end
                return a.bitcast(F32) if dtype == F32 else a
            T1b = [T1] + [carve_p1(512, F32) for _ in range(NB2 - 1)]; T1bR = [T1R] + [Res("t1b%d" % i) for i in range(NB2 - 1)]
            T3b = [T3] + [carve_p1(512) for _ in range(NB2 - 1)]; T3bR = [T3R] + [Res("t3b%d" % i) for i in range(NB2 - 1)]
            SZb = [SZ] + [carve(512) for _ in range(NB2 - 1)]; SZbR = [SZR] + [Res("szb%d" % i) for i in range(NB2 - 1)]
            S.barrier()
            Gb = [G, G2]
            GRb = [GR, [Res("g2_%d" % c) for c in range(4)]]
            hcv = hc_d
            it = 0
            i2 = 0
            pending = None
            wok_next = load_wo(L, 0)
            for g in range(4):
                Gc, GRc = Gb[g % 2], GRb[g % 2]
                sz_ = [load_w(L, 16 + 2 * g), load_w(L, 16 + 2 * g + 1)]
                for cc in range(4):
                    cg = g * 4 + cc
                    hi = it % 2; it += 1
                    S.dma("sp", lambda e, hi=hi, cg=cg: e.dma_start(out=HC[hi], in_=hcv[cg * 128:(cg + 1) * 128, :]),
                          "k_hcld%d" % hi, reads=[HCD[cg]], writes=[HCR[hi]])
                    for tb in range(4):
                        sl = slice(tb * 512, (tb + 1) * 512)
                        q = i2 % NB2; i2 += 1
                        b = bank()
                        inproj_fm(sz_[cc // 2], (cc % 2) * 128, tb, b)
                        S.op("act", lambda e, b=b, q=q: e.activation(out=SZb[q], in_=PS[b][:, :], func=AF.Silu),
                             reads=[PSR[b]], writes=[SZbR[q]])
                        S.op("pool", lambda e, hi=hi, sl=sl, q=q: e.tensor_tensor(out=T1b[q], in0=HC[hi][:, sl], in1=S1[:, sl], op=ALU.subtract),
                             reads=[HCR[hi], S1R[tb]], writes=[T1bR[q]])
                        S.op("dve", lambda e, sl=sl, q=q: e.tensor_tensor(out=T1b[q], in0=T1b[q], in1=S2[:, sl], op=ALU.mult),
                             reads=[T1bR[q], S2R[tb]], writes=[T1bR[q]])
                        S.op("act", lambda e, cg=cg, q=q: e.activation(out=T3b[q], in_=T1b[q], func=AF.Silu, scale=VEC[:, cg, V_LNG:V_LNG + 1],
                                                                      bias=VEC[:, cg, V_LNB:V_LNB + 1]),
                             reads=[T1bR[q], VECR], writes=[T3bR[q]])
                        S.op("dve", lambda e, cc=cc, sl=sl, q=q, Gc=Gc: e.tensor_tensor(out=Gc[:, cc, sl], in0=T3b[q], in1=SZb[q], op=ALU.mult),
                             reads=[T3bR[q], SZbR[q]], writes=[GRc[cc]])
                    if pending is not None and cc == 0:
                        outproj_partial(*pending)
                        pending = None
                        wok_next = load_wo(L, g)
                pending = (Gc, GRc, wok_next, list(range(NT)))
            outproj_partial(*pending)

        emitters = {0: layer0, 1: layer1, 2: layer2, 3: layer3}
        for L in layers:
            gk = load_gain(L)
            emitters[L](L, gk)
            S.barrier()

        scratch_reset()
        yv = y_d.rearrange("(t p) d -> p t d", p=128)
        if final:
            gk = load_gain(4)
            junk = carve(D); junkr = Res("junk")
            OB = [carve(D, F32) for _ in range(2)]; OBR = [Res("ob0"), Res("ob1")]
            rms_stats()
            for t in range(NT):
                S.op("act", lambda e, t=t: e.activation(out=junk, in_=X[:, t, :], func=AF.Square, accum_out=SS[:, 0, t:t + 1]),
                     reads=[XR[t], SSR], writes=[junkr, SSR])
            S.op("act", lambda e: e.activation(out=SS[:, 2, :], in_=SS[:, 0, :], func=AF.Sqrt, scale=1.0 / D, bias=EPS_T[:, 0:1]),
                 reads=[SSR, EPSR], writes=[SSR])
            S.op("dve", lambda e: e.reciprocal(out=SS[:, 1, :], in_=SS[:, 2, :]), reads=[SSR], writes=[SSR])
            for t in range(NT):
                i = t % 2
                S.op("dve", lambda e, t=t, i=i: e.scalar_tensor_tensor(out=OB[i], in0=X[:, t, :], scalar=SS[:, 1, t:t + 1],
                                                                      in1=GB[gk], op0=ALU.mult, op1=ALU.mult),
                     reads=[XR[t], SSR, GBR[gk]], writes=[OBR[i]])
                S.dma("sp", lambda e, t=t, i=i: e.dma_start(out=yv[:, t, :], in_=OB[i]), "k_y%d" % i, reads=[OBR[i]])
        else:
            for q in range(4):
                S.dma("sp", lambda e, q=q: e.dma_start(out=yv[:, 4 * q:4 * q + 4, :], in_=X[:, 4 * q:4 * q + 4, :]),
                      "k_y%d" % q, reads=XR[4 * q:4 * q + 4])
        S.emit()
    return nc


def _bf(a):
    return np.ascontiguousarray(a.astype(ml_dtypes.bfloat16))


_CONST_CACHE = {}


def _constants():
    if _CONST_CACHE:
        return _CONST_CACHE
    c = {}
    c["idb"] = _bf(np.eye(128, dtype=np.float32))
    c["onef"] = np.ones((128, 128), np.float32)
    cc = np.arange(512)
    ang = 2.0 * np.pi * ((cc[:, None] * cc[None, :]) % 512) / 512.0
    tab = np.concatenate([np.cos(ang), np.sin(ang)], axis=1) / 32.0
    c["csc"] = _bf(tab.reshape(4, 128, 1024).transpose(1, 0, 2).reshape(128, 4096))
    sv = 1 + np.arange(1024)
    kv = np.arange(3 * KB_W)
    ang = 2.0 * np.pi * ((sv[:, None] * kv[None, :]) % S_LEN) / float(S_LEN)
    cosf = np.cos(ang) / 32.0
    sinf = -np.sin(ang) / 32.0
    cosf[1023, :] *= 0.5
    sinf[1023, :] = 0.0
    cosf[:, 1025:] = 0.0
    sinf[:, 1025:] = 0.0
    full = np.stack([cosf, sinf], axis=0)
    c["css"] = _bf(full.reshape(2, 8, 128, 3, KB_W).transpose(0, 3, 1, 2, 4))
    pm = np.zeros((128, 4, 5, 128), np.float64)
    for g, w in enumerate((2, 4, 8, 16)):
        left = w // 2
        right = w - 1 - left
        M = np.zeros((S_LEN, S_LEN), np.float64)
        for t in range(S_LEN):
            lo = max(t - left, 0)
            hi = min(t + right + 1, S_LEN)
            M[lo:hi, t] = 1.0 / (hi - lo)
            M[t, t] -= 1.0
        pm[:, g, 0] = M[0:128, 0:128]
        pm[:, g, 1] = M[128:256, 128:256]
        pm[:, g, 2] = M[S_LEN - 128:, S_LEN - 128:]
        pm[:, g, 3] = M[128:256, 256:384]
        pm[:, g, 4] = M[384:512, 256:384]
    c["pm"] = _bf(pm.reshape(128, 4 * 5 * 128))
    _CONST_CACHE.update(c)
    return c


def _arr_win(w):
    ncol = w.shape[1]
    return np.ascontiguousarray(w.reshape(8, 128, ncol // 256, 256).transpose(2, 1, 0, 3))


def _prep_shared(inp):
    f = lambda k: np.asarray(inp[k], dtype=np.float32)
    sh = dict(_constants())
    sh["ng"] = np.ascontiguousarray(np.concatenate([f("norm_g"), f("final_g")[None, :]], axis=0))
    sh["wi0"] = _arr_win(f("fnet_w_in")[0])
    sh["wi1"] = _arr_win(f("conf_w_in")[0])
    sh["wi2"] = _arr_win(f("pool_w_in")[0])
    sh["wi3"] = _arr_win(f("sc_w_in")[0])
    sh["wo"] = np.ascontiguousarray(f("w_out").reshape(4, 4, 4, 128, 1024).transpose(0, 1, 3, 2, 4))
    sh["wm"] = np.ascontiguousarray(f("fnet_w_mix")[0].reshape(4, 4, 128, 512).transpose(0, 2, 1, 3))
    sh["wg"] = np.ascontiguousarray(f("pool_w_grp")[0].reshape(4, 4, 128, 512).transpose(0, 2, 1, 3))
    rows = np.zeros((NV, E), np.float32)
    rows[V_DW:V_DW + 31] = f("conf_dw_w")[0]
    rows[V_DWB] = f("conf_dw_b")[0]
    rows[V_LNG] = f("conf_ln_g")[0]
    rows[V_LNB] = f("conf_ln_b")[0]
    rows[V_PSC] = f("pool_scale")[0]
    rows[V_BMIX] = f("fnet_b_mix")[0].reshape(E)
    rows[V_SC:V_SC + 3] = f("sc_conv_w")[0]
    sh["vec"] = np.ascontiguousarray(rows.reshape(NV, 16, 128).transpose(2, 1, 0).reshape(128, 16 * NV))
    return sh


_PROG_CACHE = {}
FUSED = True


def _run(layers, final, xs, sh):
    key = (tuple(layers), final)
    if key not in _PROG_CACHE:
        _PROG_CACHE[key] = build_program(list(layers), final)
    nc = _PROG_CACHE[key]
    names = ["ng", "wo", "wm", "wg", "vec", "idb", "onef", "csc", "css", "pm"] + ["wi%d" % L for L in layers]
    in_maps = []
    for b in range(8):
        m = {n: sh[n] for n in names}
        m["x"] = np.ascontiguousarray(xs[b])
        in_maps.append(m)
    res = run_bass_kernel_spmd(nc, in_maps, core_ids=list(range(8)))
    return np.stack([np.asarray(r["y"]) for r in res.results], axis=0)


def kernel(**inputs):
    sh = _prep_shared(inputs)
    x = np.asarray(inputs["x"], dtype=np.float32)
    if FUSED:
        return _run((0, 1, 2, 3), True, x, sh).astype(np.float32)
    cur = x
    for L in range(4):
        cur = _run((L,), L == 3, cur, sh)
    return cur.astype(np.float32)
```

```python
import contextlib
import os
import numpy as np
import ml_dtypes
import concourse.bass as bass
import concourse.mybir as mybir
from concourse.bass_utils import run_bass_kernel_spmd

F32 = mybir.dt.float32
BF16 = mybir.dt.bfloat16
AF = mybir.ActivationFunctionType
ALU = mybir.AluOpType

S_LEN = 2048
D = 1024
E = 2048
NT = 16
TOT = 106000


class Res:
    __slots__ = ("name", "w", "r")

    def __init__(self, name):
        self.name = name
        self.w = None
        self.r = {}


class Sched:
    ENG = ("pe", "act", "dve", "pool", "sp")
    CE = ("pe", "act", "dve", "pool")

    def __init__(self, nc):
        self.nc = nc
        self.ops = {e: [] for e in self.ENG}
        self.cnt = {}
        self.nops = {e: 0 for e in self.CE}
        self.elig = {e: [] for e in self.CE}
        self.waited = {e: {} for e in self.ENG}

    def _deps(self, eng, reads, writes):
        need = {}
        for r in reads:
            if r.w is not None:
                k, v = r.w
                need[k] = max(need.get(k, 0), v)
        for r in writes:
            if r.w is not None:
                k, v = r.w
                need[k] = max(need.get(k, 0), v)
            for k, v in r.r.items():
                need[k] = max(need.get(k, 0), v)
        waits = []
        wd = self.waited[eng]
        for k, v in need.items():
            if eng == "pe" and k == "pe":
                continue
            if wd.get(k, 0) < v:
                wd[k] = v
                waits.append((k, v))
        return waits

    def _mark(self, key, val, reads, writes):
        for r in reads:
            r.r[key] = max(r.r.get(key, 0), val)
        for r in writes:
            r.w = (key, val)
            r.r = {}

    def op(self, eng, fn, reads=(), writes=(), signal=True):
        waits = self._deps(eng, reads, writes)
        self.nops[eng] += 1
        seq = self.nops[eng]
        if signal:
            self.elig[eng].append(seq)
        self.ops[eng].append((waits, fn, (eng, seq)))
        self._mark(eng, seq, reads, writes)
        return seq

    def dma(self, q, fn, key, reads=(), writes=()):
        if key not in self.cnt:
            self.cnt[key] = 0
        waits = self._deps(q, reads, writes)
        self.cnt[key] += 16
        val = self.cnt[key]
        self.ops[q].append((waits, fn, (key, 16)))
        self._mark(key, val, reads, writes)

    def barrier(self):
        for e in self.ENG:
            waits = []
            tgt = dict(self.cnt)
            tgt.update(self.nops)
            for k, v in tgt.items():
                if v > 0 and self.waited[e].get(k, 0) < v and not (e == "pe" and k == "pe"):
                    self.waited[e][k] = v
                    waits.append((k, v))
            if waits:
                self.ops[e].append((waits, None, None))

    def resolve(self):
        import bisect
        needed = {e: set() for e in self.CE}
        for e in self.ENG:
            for waits, fn, inc in self.ops[e]:
                for k, v in waits:
                    if k in needed:
                        el = self.elig[k]
                        i = bisect.bisect_left(el, v)
                        assert i < len(el), ("no signalling op after", k, v)
                        needed[k].add(el[i])
        rank = {e: {q: i + 1 for i, q in enumerate(sorted(needed[e]))} for e in self.CE}
        order = {e: sorted(needed[e]) for e in self.CE}

        def translate(k, v):
            if k not in rank:
                return v
            i = bisect.bisect_left(order[k], v)
            return rank[k][order[k][i]]
        return rank, translate

    def emit(self):
        nc = self.nc
        with contextlib.ExitStack() as st:
            self.barrier()
            rank, translate = self.resolve()
            sems = {k: st.enter_context(nc.semaphore("s_" + k)) for k in list(self.cnt) + list(self.CE)}
            block = st.enter_context(nc.Block())

            def body(e):
                def run(eng):
                    for waits, fn, inc in self.ops[e]:
                        for k, v in waits:
                            eng.wait_ge(sems[k], translate(k, v))
                        if fn is None:
                            continue
                        ins = fn(eng)
                        if inc is None:
                            continue
                        if inc[0] in rank:
                            if inc[1] in rank[inc[0]]:
                                ins.then_inc(sems[inc[0]], 1)
                        else:
                            ins.then_inc(sems[inc[0]], inc[1])
                return run

            block.tensor(body("pe"))
            block.scalar(body("act"))
            block.vector(body("dve"))
            block.gpsimd(body("pool"))
            block.sync(body("sp"))


WIN_COLS = {0: 4096, 1: 6144, 2: 4096, 3: 8192}
V_DW, V_DWB, V_LNG, V_LNB, V_PSC, V_BMIX, V_SC = 0, 31, 32, 33, 34, 35, 36
NV = 40
_STOP = int(os.environ.get('L0_STOP', '9'))
KB_W = 342
KB0 = (0, 342, 684)
KBN = (342, 342, 341)


def build_program(layers, final):
    nc = bass.Bass("TRN2", target_bir_lowering=False)
    dt = nc.dram_tensor
    x_d = dt("x", [S_LEN, D], F32, kind="ExternalInput").ap()
    ng_d = dt("ng", [5, D], F32, kind="ExternalInput").ap()
    wi_d = {L: dt("wi%d" % L, [WIN_COLS[L] // 256, 128, 8, 256], F32, kind="ExternalInput").ap() for L in layers}
    wo_d = dt("wo", [4, 4, 128, 4, 1024], F32, kind="ExternalInput").ap()
    wm_d = dt("wm", [4, 128, 4, 512], F32, kind="ExternalInput").ap()
    wg_d = dt("wg", [4, 128, 4, 512], F32, kind="ExternalInput").ap()
    vec_d = dt("vec", [128, 16 * NV], F32, kind="ExternalInput").ap()
    idb_d = dt("idb", [128, 128], BF16, kind="ExternalInput").ap()
    onef_d = dt("onef", [128, 128], F32, kind="ExternalInput").ap()
    csc_d = dt("csc", [128, 4 * 1024], BF16, kind="ExternalInput").ap()
    css_d = dt("css", [2, 3, 8, 128, KB_W], BF16, kind="ExternalInput").ap()
    pm_d = dt("pm", [128, 4 * 5 * 128], BF16, kind="ExternalInput").ap()
    y_d = dt("y", [S_LEN, D], F32, kind="ExternalOutput").ap()
    hc_d = dt("hc_scr", [E, S_LEN], BF16, kind="Internal").ap() if 1 in layers else None

    with contextlib.ExitStack() as st:
        big = st.enter_context(nc.sbuf_tensor("big", [128, TOT], BF16))
        PS = [st.enter_context(nc.psum_tensor("ps%d" % i, [128, 512], F32)) for i in range(8)]
        PSR = [Res("ps%d" % i) for i in range(8)]
        S = Sched(nc)

        off = [0]

        def carve(n, dtype=BF16):
            units = n * (2 if dtype == F32 else 1)
            a = big[:, off[0]:off[0] + units]
            off[0] += units
            assert off[0] <= TOT, off[0]
            return a.bitcast(F32) if dtype == F32 else a

        X = carve(NT * D, F32).rearrange("p (t d) -> p t d", t=NT)
        XR = [Res("x%d" % t) for t in range(NT)]
        XNT = carve(8 * S_LEN).rearrange("p (c s) -> p c s", c=8)
        XNTR = [Res("xnt%d" % t) for t in range(NT)]
        Gfull = carve(4 * S_LEN)
        G = Gfull.rearrange("p (c s) -> p c s", c=4)
        GR = [Res("g%d" % c) for c in range(4)]
        NWB = 6
        WB = [carve(8 * 256).rearrange("p (c n) -> p c n", c=8) for _ in range(NWB)]
        WBR = [Res("wb%d" % i) for i in range(NWB)]
        WO = [carve(4 * 1024).rearrange("p (c n) -> p c n", c=4) for _ in range(1)]
        WOR = [Res("wo%d" % i) for i in range(1)]
        IDB = carve(128); IDBR = Res("idb")
        ONEF = carve(128, F32); ONEFR = Res("onef")
        VEC = carve(16 * NV, F32).rearrange("p (c r) -> p c r", c=16); VECR = Res("vec")
        GB = [carve(D, F32) for _ in range(1)]
        GBR = [Res("gb%d" % i) for i in range(1)]
        SS = carve(4 * NT, F32).rearrange("p (a t) -> p a t", a=4); SSR = Res("ss")
        scratch0 = off[0]

        wb_rr = [0]

        def add_w_slots(n):
            del WB[NWB:]; del WBR[NWB:]
            for i in range(n):
                WB.append(carve(8 * 256).rearrange("p (c n) -> p c n", c=8))
                WBR.append(Res("wbx%d" % i))

        def load_w(L, chunk):
            k = wb_rr[0] % len(WB)
            wb_rr[0] += 1
            dst = WB[k]
            S.dma("pool", lambda e, dst=dst: e.dma_start(out=dst, in_=wi_d[L][chunk]), "k_wb%d" % k, writes=[WBR[k]])
            return k

        wo_rr = [0]

        def load_wo(L, g, after=()):
            k = 0
            S.dma("pool", lambda e, k=k: e.dma_start(out=WO[k], in_=wo_d[L, g]), "k_wo%d" % k, reads=list(after), writes=[WOR[k]])
            return k

        bank_rr = [0]

        def bank():
            b = bank_rr[0] % 8
            bank_rr[0] += 1
            return b

        def mm_group(b, out_ap, pairs, reads):
            n = len(pairs)
            for i, (l, r) in enumerate(pairs):
                S.op("pe", lambda e, l=l, r=r, i=i: e.matmul(out_ap, l, r, start=(i == 0), stop=(i == n - 1)),
                     reads=reads, writes=[PSR[b]], signal=(i == n - 1))

        def mm_multi(b, groups, reads):
            tot = sum(len(p) for _, p in groups)
            idx = 0
            for out_ap, pairs in groups:
                n = len(pairs)
                for i, (l, r) in enumerate(pairs):
                    S.op("pe", lambda e, o=out_ap, l=l, r=r, f=(idx == 0), la=(i == n - 1): e.matmul(
                        o, l, r, start=f, stop=la, skip_group_check=True),
                        reads=reads, writes=[PSR[b]], signal=(idx == tot - 1))
                    idx += 1

        S.dma("sp", lambda e: e.dma_start(out=IDB, in_=idb_d[:, :]), "k_c0", writes=[IDBR])
        S.dma("sp", lambda e: e.dma_start(out=ONEF, in_=onef_d[:, :]), "k_c1", writes=[ONEFR])
        S.dma("sp", lambda e: e.dma_start(out=VEC.rearrange("p c r -> p (c r)"), in_=vec_d[:, :]), "k_c2", writes=[VECR])
        xv = x_d.rearrange("(t p) d -> p t d", p=128)
        for q in range(4):
            S.dma("sp", lambda e, q=q: e.dma_start(out=X[:, 4 * q:4 * q + 4, :], in_=xv[:, 4 * q:4 * q + 4, :]),
                  "k_x%d" % q, writes=XR[4 * q:4 * q + 4])
        gb_rr = [0]

        def load_gain(row):
            k = 0
            S.dma("sp", lambda e, k=k: e.dma_start(out=GB[k], in_=ng_d[row:row + 1, :].partition_broadcast(128)),
                  "k_gb%d" % k, writes=[GBR[k]])
            return k

        def rms_stats():
            S.op("dve", lambda e: e.memset(SS[:, 0, :], 0.0), writes=[SSR])

        def norm_and_transpose(gk, junk, junkr, xn_buf, xn_res):
            for t in range(NT):
                S.op("act", lambda e, t=t: e.activation(out=junk, in_=X[:, t, :], func=AF.Square,
                                                        accum_out=SS[:, 0, t:t + 1]),
                     reads=[XR[t], SSR], writes=[junkr, SSR])
            S.op("act", lambda e: e.activation(out=SS[:, 2, :], in_=SS[:, 0, :], func=AF.Sqrt, scale=1.0 / D, bias=EPS_T[:, 0:1]),
                 reads=[SSR, EPSR], writes=[SSR])
            S.op("dve", lambda e: e.reciprocal(out=SS[:, 1, :], in_=SS[:, 2, :]), reads=[SSR], writes=[SSR])
            for t in range(NT):
                i = t % 2
                S.op("dve", lambda e, t=t, i=i: e.scalar_tensor_tensor(out=xn_buf[i], in0=X[:, t, :], scalar=SS[:, 1, t:t + 1],
                                                                      in1=GB[gk], op0=ALU.mult, op1=ALU.mult),
                     reads=[XR[t], SSR, GBR[gk]], writes=[xn_res[i]])
                b = bank()
                psT = PS[b][:, :].bitcast(BF16).rearrange("p (a b) -> p a b", a=8)
                for c in range(8):
                    S.op("pe", lambda e, c=c, i=i, psT=psT: e.transpose(out=psT[:, c, :], in_=xn_buf[i][:, c * 128:(c + 1) * 128],
                                                                       identity=IDB),
                         reads=[xn_res[i], IDBR], writes=[PSR[b]], signal=(c == 7))
                eng = "act" if t % 2 == 0 else "dve"
                if eng == "act":
                    S.op("act", lambda e, t=t, psT=psT: e.activation(out=XNT[:, :, t * 128:(t + 1) * 128], in_=psT, func=AF.Copy),
                         reads=[PSR[b]], writes=[XNTR[t]])
                else:
                    S.op("dve", lambda e, t=t, psT=psT: e.tensor_copy(out=XNT[:, :, t * 128:(t + 1) * 128], in_=psT),
                         reads=[PSR[b]], writes=[XNTR[t]])

        def inproj_fm(slot, col0, tb, out_bank):
            pairs = [(WB[slot][:, dc, col0:col0 + 128], XNT[:, dc, tb * 512:(tb + 1) * 512]) for dc in range(8)]
            mm_group(out_bank, PS[out_bank][:, :], pairs, [WBR[slot]] + XNTR[4 * tb:4 * tb + 4])

        def inproj_tm(slots, t, evac):
            for h, sl in enumerate(slots):
                b = bank()
                pairs = [(XNT[:, dc, t * 128:(t + 1) * 128], WB[sl][:, dc, :]) for dc in range(8)]
                mm_group(b, PS[b][:, 0:256], pairs, [WBR[sl], XNTR[t]])
                evac(b, h)

        def outproj_partial(gbuf, gres, wok, tiles):
            for j, t in enumerate(tiles):
                for dh in range(2):
                    b = bank()
                    pairs = [(gbuf[:, cc, j * 128:(j + 1) * 128], WO[wok][:, cc, dh * 512:(dh + 1) * 512]) for cc in range(4)]
                    mm_group(b, PS[b][:, :], pairs, list(gres) + [WOR[wok]])
                    S.op("dve", lambda e, t=t, dh=dh, b=b: e.tensor_tensor(out=X[:, t, dh * 512:(dh + 1) * 512],
                                                                           in0=X[:, t, dh * 512:(dh + 1) * 512],
                                                                           in1=PS[b][:, :], op=ALU.add),
                         reads=[PSR[b], XR[t]], writes=[XR[t]])

        EPS_T = carve(2, F32); EPSR = Res("eps")
        S.op("dve", lambda e: e.memset(EPS_T[:, 0:1], 1e-6), writes=[EPSR])
        S.op("dve", lambda e: e.memset(EPS_T[:, 1:2], 1e-5), writes=[EPSR])
        scratch0 = off[0]

        def scratch_reset():
            off[0] = scratch0

        def layer_common_begin(L, gk):
            scratch_reset()
            del WB[NWB:]; del WBR[NWB:]
            junk = Gfull[:, 0:D]; junkr = Res("junk")
            xn_buf = [Gfull[:, D:2 * D], Gfull[:, 2 * D:3 * D]]
            xn_res = [Res("xn0"), Res("xn1")]
            rms_stats()
            norm_and_transpose(gk, junk, junkr, xn_buf, xn_res)
            scratch_reset()

        def layer3(L, gk):
            layer_common_begin(L, gk)
            CH = [carve(2080) for _ in range(2)]; CHR = [Res("ch0"), Res("ch1")]
            BZ = [carve(2048) for _ in range(2)]; BZR = [Res("bz0"), Res("bz1")]
            ACC = carve(2048, F32); ACCR = Res("acc")
            SZ = [carve(512) for _ in range(2)]; SZR = [Res("sz0"), Res("sz1")]
            HS = [carve(512) for _ in range(2)]; HSR = [Res("hs0"), Res("hs1")]
            for i in range(2):
                S.op("dve", lambda e, i=i: e.memset(CH[i], 0.0), writes=[CHR[i]])
            G2 = carve(4 * S_LEN).rearrange("p (c s) -> p c s", c=4)
            Gb = [G, G2]
            GRb = [GR, [Res("g2_%d" % c) for c in range(4)]]
            add_w_slots(2)

            def conv_block(pc, n):
                pci, pcg, pcc, pG, pGR = pc
                c0 = n * 512
                S.op("dve", lambda e: e.tensor_scalar_mul(out=ACC[:, c0:c0 + 512], in0=CH[pci][:, c0:c0 + 512],
                                                         scalar1=VEC[:, pcg, V_SC:V_SC + 1]),
                     reads=[CHR[pci], VECR], writes=[ACCR])
                for j in (1, 2):
                    S.op("dve", lambda e, j=j: e.scalar_tensor_tensor(
                        out=ACC[:, c0:c0 + 512], in0=CH[pci][:, c0 + j:c0 + j + 512], scalar=VEC[:, pcg, V_SC + j:V_SC + j + 1],
                        in1=ACC[:, c0:c0 + 512], op0=ALU.mult, op1=ALU.add), reads=[CHR[pci], VECR, ACCR], writes=[ACCR])
                S.op("dve", lambda e: e.tensor_tensor(out=pG[:, pcc, c0:c0 + 512], in0=ACC[:, c0:c0 + 512],
                                                      in1=BZ[pci][:, c0:c0 + 512], op=ALU.mult),
                     reads=[ACCR, BZR[pci]], writes=[pGR[pcc]])

            it = 0
            pending = None
            conv_todo = None
            for g in range(4):
                Gc, GRc = Gb[g % 2], GRb[g % 2]
                for half in range(2):
                    ch = g * 2 + half
                    sl = [load_w(L, kind * 8 + ch) for kind in range(4)]
                    for c2 in range(2):
                        cc = half * 2 + c2
                        cg = g * 4 + cc
                        ci = cg % 2
                        col0 = c2 * 128
                        for tb in range(4):
                            i2 = it % 2; it += 1
                            bz_, bh_, bc_, bb_ = bank(), bank(), bank(), bank()
                            inproj_fm(sl[3], col0, tb, bz_)
                            inproj_fm(sl[2], col0, tb, bh_)
                            inproj_fm(sl[1], col0, tb, bc_)
                            inproj_fm(sl[0], col0, tb, bb_)
                            S.op("act", lambda e, b=bz_, i2=i2: e.activation(out=SZ[i2], in_=PS[b][:, :], func=AF.Silu),
                                 reads=[PSR[bz_]], writes=[SZR[i2]])
                            S.op("act", lambda e, b=bh_, i2=i2: e.activation(out=HS[i2], in_=PS[b][:, :], func=AF.Copy),
                                 reads=[PSR[bh_]], writes=[HSR[i2]])
                            S.op("dve", lambda e, b=bc_, i2=i2, ci=ci, tb=tb: e.tensor_tensor(
                                out=CH[ci][:, 1 + tb * 512:1 + (tb + 1) * 512], in0=PS[b][:, :], in1=HS[i2], op=ALU.mult),
                                reads=[PSR[bc_], HSR[i2]], writes=[CHR[ci]])
                            S.op("dve", lambda e, b=bb_, i2=i2, ci=ci, tb=tb: e.tensor_tensor(
                                out=BZ[ci][:, tb * 512:(tb + 1) * 512], in0=PS[b][:, :], in1=SZ[i2], op=ALU.mult),
                                reads=[PSR[bb_], SZR[i2]], writes=[BZR[ci]])
                            if conv_todo is not None:
                                conv_block(conv_todo, tb)
                        conv_todo = (ci, cg, cc, Gc, GRc)
                        if g == 0 and cc == 1:
                            wok_next = load_wo(L, 0, after=[BZR[ci]])
                        if pending is not None and cc == 0:
                            outproj_partial(*pending)
                            pending = None
                            wok_next = load_wo(L, g)
                pending = (Gc, GRc, wok_next, list(range(NT)))
            for n in range(4):
                conv_block(conv_todo, n)
            outproj_partial(*pending)

        def mix_tail(L, g, kb, wsl_z, WM, WMR, FT, FTR, SZK, SZKR, wok, bias_row, scale_row):
            for cc in range(4):
                b = bank()
                inproj_fm(wsl_z[cc // 2], (cc % 2) * 128, kb, b)
                S.op("act", lambda e, b=b, cc=cc: e.activation(out=SZK[:, cc, :], in_=PS[b][:, :], func=AF.Silu),
                     reads=[PSR[b]], writes=[SZKR[cc]])
            for dt_ in range(4):
                b = bank()
                pairs = [(WM[:, lc, dt_ * 128:(dt_ + 1) * 128], FT[:, lc, :]) for lc in range(4)]
                mm_group(b, PS[b][:, :], pairs, [WMR] + FTR)
                cg = g * 4 + dt_
                if bias_row is not None:
                    S.op("dve", lambda e, b=b, dt_=dt_, cg=cg: e.scalar_tensor_tensor(
                        out=SZK[:, dt_, :], in0=PS[b][:, :], scalar=VEC[:, cg, bias_row:bias_row + 1], in1=SZK[:, dt_, :],
                        op0=ALU.add, op1=ALU.mult), reads=[PSR[b], VECR, SZKR[dt_]], writes=[SZKR[dt_]])
                else:
                    S.op("dve", lambda e, b=b, dt_=dt_, cg=cg: e.scalar_tensor_tensor(
                        out=SZK[:, dt_, :], in0=PS[b][:, :], scalar=VEC[:, cg, scale_row:scale_row + 1], in1=SZK[:, dt_, :],
                        op0=ALU.mult, op1=ALU.mult), reads=[PSR[b], VECR, SZKR[dt_]], writes=[SZKR[dt_]])
            return (SZK, SZKR, wok, [4 * kb + j for j in range(4)])

        def load_group_mat(dst, dst_res, src, g, key):
            S.dma("pool", lambda e: e.dma_start(out=dst, in_=src[g]), key, writes=[dst_res])

        def layer0(L, gk):
            layer_common_begin(L, gk)
            UG = G.rearrange("p c s -> p (c s)").rearrange("p (t n) -> p t n", t=NT)
            UGR = [Res("ug%d" % t) for t in range(NT)]
            CSC = carve(4 * 1024).rearrange("p (c n) -> p c n", c=4); CSCR = Res("csc")
            WM = carve(4 * 512).rearrange("p (c n) -> p c n", c=4); WMR = Res("wm")
            NCSS = 6
            CSS = [carve(352) for _ in range(NCSS)]; CSSR = [Res("css%d" % i) for i in range(NCSS)]
            PT = [carve(4 * 352).rearrange("p (c n) -> p c n", c=4) for _ in range(2)]
            PTR = [[Res("pt%d_%d" % (a, c)) for c in range(4)] for a in range(2)]
            FTF = carve(4 * S_LEN).rearrange("p (c n) -> p c n", c=4); FTR = [Res("ft%d" % c) for c in range(4)]
            SZKb = [carve(4 * 512).rearrange("p (c n) -> p c n", c=4) for _ in range(2)]
            SZKRb = [[Res("szk%d_%d" % (i, c)) for c in range(4)] for i in range(2)]
            FE = [[carve(8 * 128).rearrange("p (c n) -> p c n", c=8) for _ in range(2)] for _ in range(1)]
            FER = [[Res("fe%d_%d" % (i, j)) for j in range(2)] for i in range(1)]
            BS = [carve(352, F32) for _ in range(2)]; BSR = [Res("bs0"), Res("bs1")]
            U0 = carve(512); U0R = Res("u0")
            ONER = carve(512); ONERR = Res("oner")
            S.op("dve", lambda e: e.memset(ONER, 1.0 / 32.0), writes=[ONERR])
            S.dma("sp", lambda e: e.dma_start(out=CSC.rearrange("p c n -> p (c n)"), in_=csc_d[:, :]), "k_csc", writes=[CSCR])
            css_i = 0
            fe_i = 0
            bs_i = 0
            for g in range(4):
                su = [load_w(L, 2 * g), load_w(L, 2 * g + 1)]
                sz_ = [load_w(L, 8 + 2 * g), load_w(L, 8 + 2 * g + 1)]
                load_group_mat(WM, WMR, wm_d, g, "k_wm")
                wok = load_wo(L, g)
                for h, sl in enumerate(su):
                    b = bank()
                    pairs = [(XNT[:, dc, 0:1], WB[sl][:, dc, :]) for dc in range(8)]
                    mm_group(b, PS[b][0:1, 0:256], pairs, [WBR[sl], XNTR[0]])
                    S.op("act", lambda e, b=b, h=h: e.activation(out=U0[0:1, h * 256:(h + 1) * 256], in_=PS[b][0:1, 0:256], func=AF.Copy),
                         reads=[PSR[b]], writes=[U0R])
                if _STOP == 1:
                    continue
                for tau in range(8):
                    fb = 0
                    lo_ = 1 + 128 * tau
                    hi_ = 128 * (16 - tau) - 1
                    rd = [XNTR[tau], XNTR[min(tau + 1, NT - 1)], XNTR[15 - tau]]
                    S.op("dve", lambda e, fb=fb, lo_=lo_, hi_=hi_: e.tensor_tensor(
                        out=FE[fb][0], in0=XNT[:, :, lo_:lo_ + 128], in1=XNT[:, :, hi_:hi_ - 128:-1], op=ALU.add),
                        reads=rd, writes=[FER[fb][0]])
                    S.op("dve", lambda e, fb=fb, lo_=lo_, hi_=hi_: e.tensor_tensor(
                        out=FE[fb][1], in0=XNT[:, :, lo_:lo_ + 128], in1=XNT[:, :, hi_:hi_ - 128:-1], op=ALU.subtract),
                        reads=rd, writes=[FER[fb][1]])
                    for eo in range(2):
                        t = 8 * eo + tau
                        for h, sl in enumerate(su):
                            b = bank()
                            pairs = [(FE[fb][eo][:, dc, :], WB[sl][:, dc, :]) for dc in range(8)]
                            mm_group(b, PS[b][:, 0:256], pairs, [WBR[sl], FER[fb][eo]])
                            if h == 0:
                                S.op("act", lambda e, b=b, t=t: e.activation(out=UG[:, t, 0:256], in_=PS[b][:, 0:256], func=AF.Copy),
                                     reads=[PSR[b]], writes=[UGR[t]])
                            else:
                                S.op("dve", lambda e, b=b, t=t: e.tensor_copy(out=UG[:, t, 256:512], in_=PS[b][:, 0:256]),
                                     reads=[PSR[b]], writes=[UGR[t]])
                if _STOP == 2:
                    continue
                for kb in range(3):
                    k0, nk = KB0[kb], KBN[kb]
                    for part in range(2):
                        banks = [bank() for _ in range(4)]
                        for tau in range(8):
                            k = css_i % NCSS; css_i += 1
                            S.dma("sp", lambda e, k=k, part=part, kb=kb, tau=tau: e.dma_start(out=CSS[k][:, 0:KB_W], in_=css_d[part, kb, tau]),
                                  "k_css%d" % k, writes=[CSSR[k]])
                            last = (tau == 7) and part == 1
                            for cc in range(4):
                                b = banks[cc]
                                S.op("pe", lambda e, b=b, cc=cc, tau=tau, k=k, part=part, last=last: e.matmul(
                                    PS[b][:, 0:KB_W], UG[:, 8 * part + tau, cc * 128:(cc + 1) * 128], CSS[k][:, 0:KB_W],
                                    start=(tau == 0), stop=last),
                                    reads=[UGR[8 * part + tau], CSSR[k]], writes=[PSR[b]], signal=True)
                        if part == 0:
                            for cc in range(4):
                                b = banks[cc]
                                S.op("pe", lambda e, b=b, cc=cc: e.matmul(
                                    PS[b][:, 0:KB_W], U0[0:1, cc * 128:(cc + 1) * 128], ONER[0:1, 0:KB_W], start=False, stop=True),
                                    reads=[U0R, ONERR], writes=[PSR[b]], signal=True)
                        for cc in range(4):
                            b = banks[cc]
                            if cc % 2 == 0:
                                S.op("act", lambda e, b=b, cc=cc, part=part: e.activation(out=PT[part][:, cc, 0:KB_W], in_=PS[b][:, 0:KB_W], func=AF.Copy),
                                     reads=[PSR[b]], writes=[PTR[part][cc]])
                            else:
                                S.op("dve", lambda e, b=b, cc=cc, part=part: e.tensor_copy(out=PT[part][:, cc, 0:KB_W], in_=PS[b][:, 0:KB_W]),
                                     reads=[PSR[b]], writes=[PTR[part][cc]])
                    if _STOP == 3:
                        continue
                    for lt in range(4):
                        ba, bb_ = bank(), bank()
                        mm_group(ba, PS[ba][:, 0:KB_W], [(CSC[:, cc, lt * 128:(lt + 1) * 128], PT[0][:, cc, 0:KB_W]) for cc in range(4)],
                                 [CSCR] + PTR[0])
                        mm_group(bb_, PS[bb_][:, 0:KB_W], [(CSC[:, cc, 512 + lt * 128:512 + (lt + 1) * 128], PT[1][:, cc, 0:KB_W]) for cc in range(4)],
                                 [CSCR] + PTR[1])
                        bi = bs_i % 2; bs_i += 1
                        S.op("act", lambda e, b=bb_, bi=bi: e.activation(out=BS[bi][:, 0:KB_W], in_=PS[b][:, 0:KB_W], func=AF.Copy),
                             reads=[PSR[bb_]], writes=[BSR[bi]])
                        S.op("dve", lambda e, b=ba, bi=bi, lt=lt, k0=k0, nk=nk: e.tensor_tensor(
                            out=FTF[:, lt, k0:k0 + nk], in0=PS[b][:, 0:nk], in1=BS[bi][:, 0:nk], op=ALU.add),
                            reads=[PSR[ba], BSR[bi]], writes=[FTR[lt]])
                        ka = max(k0, 1)
                        kz = min(k0 + nk, 1024)
                        S.op("dve", lambda e, b=ba, bi=bi, lt=lt, k0=k0, ka=ka, kz=kz: e.tensor_tensor(
                            out=FTF[:, lt, S_LEN - ka:S_LEN - kz:-1], in0=PS[b][:, ka - k0:kz - k0], in1=BS[bi][:, ka - k0:kz - k0],
                            op=ALU.subtract), reads=[PSR[ba], BSR[bi]], writes=[FTR[lt]])
                if _STOP == 4:
                    continue
                pend = None
                for tb in range(4):
                    nxt = mix_tail(L, g, tb, sz_, WM, WMR, FTF[:, :, tb * 512:(tb + 1) * 512], FTR, SZKb[tb % 2], SZKRb[tb % 2], wok, V_BMIX, None)
                    if pend is not None:
                        outproj_partial(*pend)
                    pend = nxt
                outproj_partial(*pend)

        def layer2(L, gk):
            layer_common_begin(L, gk)
            UG = G.rearrange("p c s -> p (c s)").rearrange("p (t n) -> p t n", t=NT)
            UGR = [Res("ug%d" % t) for t in range(NT)]
            PM = carve(4 * 5 * 128).rearrange("p (g k n) -> p g k n", g=4, k=5); PMR = Res("pm")
            WM = carve(4 * 512).rearrange("p (c n) -> p c n", c=4); WMR = Res("wm")
            FT = carve(4 * 512).rearrange("p (c n) -> p c n", c=4); FTR = [Res("ft%d" % c) for c in range(4)]
            SZKb = [carve(4 * 512).rearrange("p (c n) -> p c n", c=4) for _ in range(2)]
            SZKRb = [[Res("szk%d_%d" % (i, c)) for c in range(4)] for i in range(2)]
            pend = None
            S.dma("sp", lambda e: e.dma_start(out=PM.rearrange("p g k n -> p (g k n)"), in_=pm_d[:, :]), "k_pm", writes=[PMR])
            for g in range(4):
                su = [load_w(L, 2 * g), load_w(L, 2 * g + 1)]
                sz_ = [load_w(L, 8 + 2 * g), load_w(L, 8 + 2 * g + 1)]
                load_group_mat(WM, WMR, wg_d, g, "k_wm")
                wok = load_wo(L, g)
                for t in range(NT):
                    def evac(b, h, t=t):
                        if h == 0:
                            S.op("act", lambda e: e.activation(out=UG[:, t, 0:256], in_=PS[b][:, 0:256], func=AF.Copy),
                                 reads=[PSR[b]], writes=[UGR[t]])
                        else:
                            S.op("dve", lambda e: e.tensor_copy(out=UG[:, t, 256:512], in_=PS[b][:, 0:256]),
                                 reads=[PSR[b]], writes=[UGR[t]])
                    inproj_tm(su, t, evac)
                for kb in range(4):
                    for cc in range(4):
                        for j in range(4):
                            T = 4 * kb + j
                            b = bank()
                            pairs = []
                            rd = [PMR]
                            for sc in (T - 1, T, T + 1):
                                if sc < 0 or sc >= NT:
                                    continue
                                if sc == T:
                                    blk = 0 if T == 0 else (2 if T == NT - 1 else 1)
                                elif sc == T - 1:
                                    blk = 3
                                else:
                                    blk = 4
                                pairs.append((UG[:, sc, cc * 128:(cc + 1) * 128], PM[:, g, blk, :]))
                                rd.append(UGR[sc])
                            mm_group(b, PS[b][:, 0:128], pairs, rd)
                            if (cc + j) % 2 == 0:
                                S.op("act", lambda e, b=b, cc=cc, j=j: e.activation(out=FT[:, cc, j * 128:(j + 1) * 128], in_=PS[b][:, 0:128], func=AF.Copy),
                                     reads=[PSR[b]], writes=[FTR[cc]])
                            else:
                                S.op("dve", lambda e, b=b, cc=cc, j=j: e.tensor_copy(out=FT[:, cc, j * 128:(j + 1) * 128], in_=PS[b][:, 0:128]),
                                     reads=[PSR[b]], writes=[FTR[cc]])
                    nxt = mix_tail(L, g, kb, sz_, WM, WMR, FT, FTR, SZKb[kb % 2], SZKRb[kb % 2], wok, None, V_PSC)
                    if pend is not None:
                        outproj_partial(*pend)
                    pend = nxt
                outproj_partial(*pend)
                pend = None

        def layer1(L, gk):
            layer_common_begin(L, gk)
            p1_base = off[0]
            HP = carve(2080); HPR = Res("hp")
            DG = carve(31 * 128).rearrange("p (j n) -> p j n", j=31); DGR = Res("dg")
            AD = [carve(512, F32) for _ in range(2)]; ADR = [Res("ad0"), Res("ad1")]
            APL = [carve(512, F32) for _ in range(2)]; APLR = [Res("ap0"), Res("ap1")]
            TH = carve(512); THR = Res("th")
            SQ = carve(512, F32); SQR = Res("sq")
            assert off[0] - p1_base >= 4 * S_LEN
            p1_end = off[0]
            G2 = big[:, p1_base:p1_base + 4 * S_LEN].rearrange("p (c s) -> p c s", c=4)
            S1 = carve(2048, F32); S1R = [Res("s1_%d" % i) for i in range(4)]
            S2 = carve(2048, F32); S2R = [Res("s2_%d" % i) for i in range(4)]
            HC = [carve(2048) for _ in range(2)]; HCR = [Res("hc0"), Res("hc1")]
            T1 = carve(512, F32); T1R = Res("t1")
            T3 = carve(512); T3R = Res("t3")
            SZ = carve(512); SZR = Res("sz")
            HCD = [Res("hcd%d" % i) for i in range(16)]
            PE_TAPS = list(range(0, 31)); DVE_TAPS = []; POOL_TAPS = []
            tbi = 0
            S.op("dve", lambda e: e.memset(HP, 0.0), writes=[HPR])
            S.op("dve", lambda e: e.memset(S1, 0.0), writes=S1R)
            S.op("dve", lambda e: e.memset(S2, 0.0), writes=S2R)
            idb_b = IDB.unsqueeze(1).to_broadcast([128, 31, 128])
            for ch in range(8):
                sa = load_w(L, ch)
                sg = load_w(L, 8 + ch)
                for c2 in range(2):
                    cg = ch * 2 + c2
                    col0 = c2 * 128
                    hi = cg % 2
                    S.op("dve", lambda e, cg=cg: e.scalar_tensor_tensor(
                        out=DG, in0=idb_b, scalar=0.5, in1=VEC[:, cg, 0:31].unsqueeze(2).to_broadcast([128, 31, 128]),
                        op0=ALU.mult, op1=ALU.mult), reads=[IDBR, VECR], writes=[DGR])
                    for tb in range(4):
                        ba, bg = bank(), bank()
                        inproj_fm(sg, col0, tb, bg)
                        inproj_fm(sa, col0, tb, ba)
                        S.op("act", lambda e, b=bg: e.activation(out=TH, in_=PS[b][:, :], func=AF.Tanh, scale=0.5),
                             reads=[PSR[bg]], writes=[THR])
                        S.op("dve", lambda e, b=ba, tb=tb: e.scalar_tensor_tensor(
                            out=HP[:, 15 + tb * 512:15 + (tb + 1) * 512], in0=TH, scalar=1.0, in1=PS[b][:, :],
                            op0=ALU.add, op1=ALU.mult), reads=[THR, PSR[ba]], writes=[HPR])
                    for tb in range(4):
                        q = tbi % 2; tbi += 1
                        for eng, taps, acc, accr in (("pool", POOL_TAPS, APL[q], APLR[q]), ("dve", DVE_TAPS, AD[q], ADR[q])):
                            for n_, j in enumerate(taps):
                                src = HP[:, tb * 512 + j:tb * 512 + j + 512]
                                if n_ == 0:
                                    S.op(eng, lambda e, src=src, acc=acc, cg=cg, j=j: e.tensor_scalar_mul(out=acc, in0=src, scalar1=VEC[:, cg, j:j + 1]),
                                         reads=[HPR, VECR], writes=[accr])
                                else:
                                    S.op(eng, lambda e, src=src, acc=acc, cg=cg, j=j: e.scalar_tensor_tensor(
                                        out=acc, in0=src, scalar=VEC[:, cg, j:j + 1], in1=acc, op0=ALU.mult, op1=ALU.add),
                                        reads=[HPR, VECR, accr], writes=[accr])
                        b = bank()
                        pairs = [(DG[:, j, :], HP[:, tb * 512 + j:tb * 512 + j + 512]) for j in PE_TAPS]
                        mm_group(b, PS[b][:, :], pairs, [DGR, HPR])
                        S.op("act", lambda e, b=b, tb=tb, hi=hi, cg=cg: e.activation(
                            out=HC[hi][:, tb * 512:(tb + 1) * 512], in_=PS[b][:, :], func=AF.Identity,
                            bias=VEC[:, cg, V_DWB:V_DWB + 1]), reads=[PSR[b], VECR], writes=[HCR[hi]])
                        S.op("act", lambda e, b=b, cg=cg: e.activation(out=SQ, in_=PS[b][:, :], func=AF.Square,
                                                                      bias=VEC[:, cg, V_DWB:V_DWB + 1]),
                             reads=[PSR[b], VECR], writes=[SQR])
                        S.op("dve", lambda e, tb=tb, hi=hi: e.tensor_tensor(out=S1[:, tb * 512:(tb + 1) * 512],
                                                                            in0=S1[:, tb * 512:(tb + 1) * 512],
                                                                            in1=HC[hi][:, tb * 512:(tb + 1) * 512], op=ALU.add),
                             reads=[HCR[hi], S1R[tb]], writes=[S1R[tb]])
                        S.op("dve", lambda e, tb=tb: e.tensor_tensor(out=S2[:, tb * 512:(tb + 1) * 512],
                                                                     in0=S2[:, tb * 512:(tb + 1) * 512], in1=SQ, op=ALU.add),
                             reads=[SQR, S2R[tb]], writes=[S2R[tb]])
                    S.dma("sp", lambda e, hi=hi, cg=cg: e.dma_start(out=hc_d[cg * 128:(cg + 1) * 128, :], in_=HC[hi]),
                          "k_hcst%d" % hi, reads=[HCR[hi]], writes=[HCD[cg]])
            for tb in range(4):
                sl = slice(tb * 512, (tb + 1) * 512)
                b1, b2 = bank(), bank()
                mm_group(b1, PS[b1][:, :], [(ONEF, S1[:, sl])], [ONEFR, S1R[tb]])
                mm_group(b2, PS[b2][:, :], [(ONEF, S2[:, sl])], [ONEFR, S2R[tb]])
                S.op("act", lambda e, b=b1, sl=sl: e.activation(out=S1[:, sl], in_=PS[b][:, :], func=AF.Copy, scale=1.0 / E),
                     reads=[PSR[b1]], writes=[S1R[tb]])
                S.op("dve", lambda e, sl=sl: e.tensor_tensor(out=T1, in0=S1[:, sl], in1=S1[:, sl], op=ALU.mult),
                     reads=[S1R[tb]], writes=[T1R])
                S.op("dve", lambda e, b=b2, sl=sl: e.scalar_tensor_tensor(out=S2[:, sl], in0=PS[b][:, :], scalar=1.0 / E, in1=T1,
                                                                          op0=ALU.mult, op1=ALU.subtract),
                     reads=[PSR[b2], T1R], writes=[S2R[tb]])
                S.op("act", lambda e, sl=sl: e.activation(out=S2[:, sl], in_=S2[:, sl], func=AF.Sqrt, bias=EPS_T[:, 1:2]),
                     reads=[S2R[tb], EPSR], writes=[S2R[tb]])
                S.op("dve", lambda e, sl=sl: e.reciprocal(out=S2[:, sl], in_=S2[:, sl]), reads=[S2R[tb]], writes=[S2R[tb]])
            NB2 = 3
            sp_ = [p1_base + 4 * S_LEN]

            def carve_p1(n, dtype=BF16):
                units = n * (2 if dtype == F32 else 1)
                a = big[:, sp_[0]:sp_[0] + units]
                sp_[0] += units
                assert sp_[0] <= p1_end
                return a.bitcast(F32) if dtype == F32 else a
            T1b = [T1] + [carve_p1(512, F32) for _ in range(NB2 - 1)]; T1bR = [T1R] + [Res("t1b%d" % i) for i in range(NB2 - 1)]
            T3b = [T3] + [carve_p1(512) for _ in range(NB2 - 1)]; T3bR = [T3R] + [Res("t3b%d" % i) for i in range(NB2 - 1)]
            SZb = [SZ] + [carve(512) for _ in range(NB2 - 1)]; SZbR = [SZR] + [Res("szb%d" % i) for i in range(NB2 - 1)]
            S.barrier()
            Gb = [G, G2]
            GRb = [GR, [Res("g2_%d" % c) for c in range(4)]]
            hcv = hc_d
            it = 0
            i2 = 0
            pending = None
            wok_next = load_wo(L, 0)
            for g in range(4):
                Gc, GRc = Gb[g % 2], GRb[g % 2]
                sz_ = [load_w(L, 16 + 2 * g), load_w(L, 16 + 2 * g + 1)]
                for cc in range(4):
                    cg = g * 4 + cc
                    hi = it % 2; it += 1
                    S.dma("sp", lambda e, hi=hi, cg=cg: e.dma_start(out=HC[hi], in_=hcv[cg * 128:(cg + 1) * 128, :]),
                          "k_hcld%d" % hi, reads=[HCD[cg]], writes=[HCR[hi]])
                    for tb in range(4):
                        sl = slice(tb * 512, (tb + 1) * 512)
                        q = i2 % NB2; i2 += 1
                        b = bank()
                        inproj_fm(sz_[cc // 2], (cc % 2) * 128, tb, b)
                        S.op("act", lambda e, b=b, q=q: e.activation(out=SZb[q], in_=PS[b][:, :], func=AF.Silu),
                             reads=[PSR[b]], writes=[SZbR[q]])
                        S.op("pool", lambda e, hi=hi, sl=sl, q=q: e.tensor_tensor(out=T1b[q], in0=HC[hi][:, sl], in1=S1[:, sl], op=ALU.subtract),
                             reads=[HCR[hi], S1R[tb]], writes=[T1bR[q]])
                        S.op("dve", lambda e, sl=sl, q=q: e.tensor_tensor(out=T1b[q], in0=T1b[q], in1=S2[:, sl], op=ALU.mult),
                             reads=[T1bR[q], S2R[tb]], writes=[T1bR[q]])
                        S.op("act", lambda e, cg=cg, q=q: e.activation(out=T3b[q], in_=T1b[q], func=AF.Silu, scale=VEC[:, cg, V_LNG:V_LNG + 1],
                                                                      bias=VEC[:, cg, V_LNB:V_LNB + 1]),
                             reads=[T1bR[q], VECR], writes=[T3bR[q]])
                        S.op("dve", lambda e, cc=cc, sl=sl, q=q, Gc=Gc: e.tensor_tensor(out=Gc[:, cc, sl], in0=T3b[q], in1=SZb[q], op=ALU.mult),
                             reads=[T3bR[q], SZbR[q]], writes=[GRc[cc]])
                    if pending is not None and cc == 0:
                        outproj_partial(*pending)
                        pending = None
                        wok_next = load_wo(L, g)
                pending = (Gc, GRc, wok_next, list(range(NT)))
            outproj_partial(*pending)

        emitters = {0: layer0, 1: layer1, 2: layer2, 3: layer3}
        for L in layers:
            gk = load_gain(L)
            emitters[L](L, gk)
            S.barrier()

        scratch_reset()
        yv = y_d.rearrange("(t p) d -> p t d", p=128)
        if final:
            gk = load_gain(4)
            junk = carve(D); junkr = Res("junk")
            OB = [carve(D, F32) for _ in range(2)]; OBR = [Res("ob0"), Res("ob1")]
            rms_stats()
            for t in range(NT):
                S.op("act", lambda e, t=t: e.activation(out=junk, in_=X[:, t, :], func=AF.Square, accum_out=SS[:, 0, t:t + 1]),
                     reads=[XR[t], SSR], writes=[junkr, SSR])
            S.op("act", lambda e: e.activation(out=SS[:, 2, :], in_=SS[:, 0, :], func=AF.Sqrt, scale=1.0 / D, bias=EPS_T[:, 0:1]),
                 reads=[SSR, EPSR], writes=[SSR])
            S.op("dve", lambda e: e.reciprocal(out=SS[:, 1, :], in_=SS[:, 2, :]), reads=[SSR], writes=[SSR])
            for t in range(NT):
                i = t % 2
                S.op("dve", lambda e, t=t, i=i: e.scalar_tensor_tensor(out=OB[i], in0=X[:, t, :], scalar=SS[:, 1, t:t + 1],
                                                                      in1=GB[gk], op0=ALU.mult, op1=ALU.mult),
                     reads=[XR[t], SSR, GBR[gk]], writes=[OBR[i]])
                S.dma("sp", lambda e, t=t, i=i: e.dma_start(out=yv[:, t, :], in_=OB[i]), "k_y%d" % i, reads=[OBR[i]])
        else:
            for q in range(4):
                S.dma("sp", lambda e, q=q: e.dma_start(out=yv[:, 4 * q:4 * q + 4, :], in_=X[:, 4 * q:4 * q + 4, :]),
                      "k_y%d" % q, reads=XR[4 * q:4 * q + 4])
        S.emit()
    return nc


def _bf(a):
    return np.ascontiguousarray(a.astype(ml_dtypes.bfloat16))


_CONST_CACHE = {}


def _constants():
    if _CONST_CACHE:
        return _CONST_CACHE
    c = {}
    c["idb"] = _bf(np.eye(128, dtype=np.float32))
    c["onef"] = np.ones((128, 128), np.float32)
    cc = np.arange(512)
    ang = 2.0 * np.pi * ((cc[:, None] * cc[None, :]) % 512) / 512.0
    tab = np.concatenate([np.cos(ang), np.sin(ang)], axis=1) / 32.0
    c["csc"] = _bf(tab.reshape(4, 128, 1024).transpose(1, 0, 2).reshape(128, 4096))
    sv = 1 + np.arange(1024)
    kv = np.arange(3 * KB_W)
    ang = 2.0 * np.pi * ((sv[:, None] * kv[None, :]) % S_LEN) / float(S_LEN)
    cosf = np.cos(ang) / 32.0
    sinf = -np.sin(ang) / 32.0
    cosf[1023, :] *= 0.5
    sinf[1023, :] = 0.0
    cosf[:, 1025:] = 0.0
    sinf[:, 1025:] = 0.0
    full = np.stack([cosf, sinf], axis=0)
    c["css"] = _bf(full.reshape(2, 8, 128, 3, KB_W).transpose(0, 3, 1, 2, 4))
    pm = np.zeros((128, 4, 5, 128), np.float64)
    for g, w in enumerate((2, 4, 8, 16)):
        left = w // 2
        right = w - 1 - left
        M = np.zeros((S_LEN, S_LEN), np.float64)
        for t in range(S_LEN):
            lo = max(t - left, 0)
            hi = min(t + right + 1, S_LEN)
            M[lo:hi, t] = 1.0 / (hi - lo)
            M[t, t] -= 1.0
        pm[:, g, 0] = M[0:128, 0:128]
        pm[:, g, 1] = M[128:256, 128:256]
        pm[:, g, 2] = M[S_LEN - 128:, S_LEN - 128:]
        pm[:, g, 3] = M[128:256, 256:384]
        pm[:, g, 4] = M[384:512, 256:384]
    c["pm"] = _bf(pm.reshape(128, 4 * 5 * 128))
    _CONST_CACHE.update(c)
    return c


def _arr_win(w):
    ncol = w.shape[1]
    return np.ascontiguousarray(w.reshape(8, 128, ncol // 256, 256).transpose(2, 1, 0, 3))


def _prep_shared(inp):
    f = lambda k: np.asarray(inp[k], dtype=np.float32)
    sh = dict(_constants())
    sh["ng"] = np.ascontiguousarray(np.concatenate([f("norm_g"), f("final_g")[None, :]], axis=0))
    sh["wi0"] = _arr_win(f("fnet_w_in")[0])
    sh["wi1"] = _arr_win(f("conf_w_in")[0])
    sh["wi2"] = _arr_win(f("pool_w_in")[0])
    sh["wi3"] = _arr_win(f("sc_w_in")[0])
    sh["wo"] = np.ascontiguousarray(f("w_out").reshape(4, 4, 4, 128, 1024).transpose(0, 1, 3, 2, 4))
    sh["wm"] = np.ascontiguousarray(f("fnet_w_mix")[0].reshape(4, 4, 128, 512).transpose(0, 2, 1, 3))
    sh["wg"] = np.ascontiguousarray(f("pool_w_grp")[0].reshape(4, 4, 128, 512).transpose(0, 2, 1, 3))
    rows = np.zeros((NV, E), np.float32)
    rows[V_DW:V_DW + 31] = f("conf_dw_w")[0]
    rows[V_DWB] = f("conf_dw_b")[0]
    rows[V_LNG] = f("conf_ln_g")[0]
    rows[V_LNB] = f("conf_ln_b")[0]
    rows[V_PSC] = f("pool_scale")[0]
    rows[V_BMIX] = f("fnet_b_mix")[0].reshape(E)
    rows[V_SC:V_SC + 3] = f("sc_conv_w")[0]
    sh["vec"] = np.ascontiguousarray(rows.reshape(NV, 16, 128).transpose(2, 1, 0).reshape(128, 16 * NV))
    return sh


_PROG_CACHE = {}
FUSED = True


def _run(layers, final, xs, sh):
    key = (tuple(layers), final)
    if key not in _PROG_CACHE:
        _PROG_CACHE[key] = build_program(list(layers), final)
    nc = _PROG_CACHE[key]
    names = ["ng", "wo", "wm", "wg", "vec", "idb", "onef", "csc", "css", "pm"] + ["wi%d" % L for L in layers]
    in_maps = []
    for b in range(8):
        m = {n: sh[n] for n in names}
        m["x"] = np.ascontiguousarray(xs[b])
        in_maps.append(m)
    res = run_bass_kernel_spmd(nc, in_maps, core_ids=list(range(8)))
    return np.stack([np.asarray(r["y"]) for r in res.results], axis=0)


def kernel(**inputs):
    sh = _prep_shared(inputs)
    x = np.asarray(inputs["x"], dtype=np.float32)
    if FUSED:
        return _run((0, 1, 2, 3), True, x, sh).astype(np.float32)
    cur = x
    for L in range(4):
        cur = _run((L,), L == 3, cur, sh)
    return cur.astype(np.float32)
```
